# Optimizing a Trainium2 kernel written in Bass

```python
import math
import jax, jax.numpy as jnp
from jax import lax
import numpy as np

D_MODEL = 2048
BATCH = 2
SEQ = 16384
DEPTH = 2

N_MIXERS = 2
N_FOX = (DEPTH + 1) // 2
N_RWKV = DEPTH // 2
FOX_HEAD_DIM = 64
FOX_HEADS = D_MODEL // FOX_HEAD_DIM
Q_BLOCK = 128
RWKV_HEAD_DIM = 64
RWKV_HEADS = D_MODEL // RWKV_HEAD_DIM
DECAY_LORA = max(32, int(round(1.8 * math.sqrt(D_MODEL) / 32)) * 32)
ICLR_LORA = max(32, int(round(1.8 * math.sqrt(D_MODEL) / 32)) * 32)
VRES_LORA = max(32, int(round(1.3 * math.sqrt(D_MODEL) / 32)) * 32)
N_SHIFT_MIX = 6
NORM_EPS = 1e-6
LNX_EPS = 64e-5

kernel_name = "fox_rwkv7_interleaved_adaln_trunk"


def rms_norm(x, w, eps=NORM_EPS):
    xf = x.astype(jnp.float32)
    y = xf * lax.rsqrt(jnp.mean(xf * xf, axis=-1, keepdims=True) + eps)
    return (y * w.astype(jnp.float32)).astype(x.dtype)


def fox_block_attention(q, k, v, cum):
    B, H, T, dh = q.shape
    scale = dh ** -0.5
    kpos = jnp.arange(T)

    def one_block(blk):
        start = blk * Q_BLOCK
        qb = lax.dynamic_slice_in_dim(q, start, Q_BLOCK, axis=2)
        cb = lax.dynamic_slice_in_dim(cum, start, Q_BLOCK, axis=2)
        s = jnp.einsum('bhqd,bhkd->bhqk', qb, k, preferred_element_type=jnp.float32) * scale
        s = s + cb[..., :, None] - cum[..., None, :]
        qpos = start + jnp.arange(Q_BLOCK)
        causal = kpos[None, :] <= qpos[:, None]
        s = jnp.where(causal, s, -jnp.inf)
        p = jax.nn.softmax(s, axis=-1)
        return jnp.einsum('bhqk,bhkd->bhqd', p.astype(v.dtype), v)

    out = lax.map(one_block, jnp.arange(T // Q_BLOCK))
    return jnp.transpose(out, (1, 2, 0, 3, 4)).reshape(B, H, T, dh)


def fox_branch(h, w_in, b_f, q_norm_w, k_norm_w, w_out):
    B, T, D = h.shape
    H, dh = FOX_HEADS, FOX_HEAD_DIM
    proj = h @ w_in
    q = proj[..., 0 * D:1 * D].reshape(B, T, H, dh)
    k = proj[..., 1 * D:2 * D].reshape(B, T, H, dh)
    v = proj[..., 2 * D:3 * D]
    z = proj[..., 3 * D:4 * D]
    f_logit = proj[..., 4 * D:] + b_f
    q = rms_norm(q, q_norm_w)
    k = rms_norm(k, k_norm_w)
    log_f = jax.nn.log_sigmoid(f_logit.astype(jnp.float32))
    cum = jnp.transpose(lax.cumsum(log_f, axis=1), (0, 2, 1))
    qh = jnp.transpose(q, (0, 2, 1, 3))
    kh = jnp.transpose(k, (0, 2, 1, 3))
    vh = jnp.transpose(v.reshape(B, T, H, dh), (0, 2, 1, 3))
    o = fox_block_attention(qh, kh, vh, cum)
    o = jnp.transpose(o, (0, 2, 1, 3)).reshape(B, T, D)
    return (o * jax.nn.silu(z)) @ w_out, v


def rwkv7_branch(h, v_first, mu, w_in, w0, w1, w2, a0, a1, a2, v0, v1, v2,
                 k_k, k_a, r_k, lnx_w, lnx_b, w_out):
    B, T, D = h.shape
    H, N = RWKV_HEADS, RWKV_HEAD_DIM
    f32 = jnp.float32
    xx = jnp.pad(h[:, :-1], ((0, 0), (1, 0), (0, 0))) - h
    xr = h + xx * mu[0]
    xw = h + xx * mu[1]
    xk = h + xx * mu[2]
    xv = h + xx * mu[3]
    xa = h + xx * mu[4]
    xg = h + xx * mu[5]
    r = xr @ w_in[0]
    k = xk @ w_in[1]
    v = xv @ w_in[2]
    z = xg @ w_in[3]
    w_log = -jax.nn.softplus(-(w0 + jnp.tanh(xw @ w1) @ w2).astype(f32)) - 0.5
    decay = jnp.exp(-jnp.exp(w_log))
    v = v + (v_first - v) * jax.nn.sigmoid(v0 + (xv @ v1) @ v2)
    a = jax.nn.sigmoid(a0 + (xa @ a1) @ a2)
    kk = (k * k_k).reshape(B, T, H, N).astype(f32)
    kk = kk / jnp.maximum(jnp.sqrt(jnp.sum(kk * kk, axis=-1, keepdims=True)), 1e-12)
    k = k * (1.0 + (a - 1.0) * k_a)

    def heads(t):
        return t.reshape(B, T, H, N).astype(f32)

    rh, wh, kh, vh, ah = heads(r), heads(decay), heads(k), heads(v), heads(a)
    a_vec = -kk
    b_vec = kk * ah

    def step(S, inp):
        r_t, w_t, k_t, v_t, a_t, b_t = inp
        sa = jnp.einsum('bhvk,bhk->bhv', S, a_t)
        S = S * w_t[:, :, None, :] + sa[..., None] * b_t[:, :, None, :] \
            + v_t[..., None] * k_t[:, :, None, :]
        y_t = jnp.einsum('bhvk,bhk->bhv', S, r_t)
        return S, y_t

    xs = tuple(jnp.moveaxis(t, 1, 0) for t in (rh, wh, kh, vh, a_vec, b_vec))
    S0 = jnp.zeros((B, H, N, N), f32)
    _, y = lax.scan(step, S0, xs)
    y = jnp.moveaxis(y, 0, 1)
    mean = jnp.mean(y, axis=-1, keepdims=True)
    var = jnp.mean(jnp.square(y - mean), axis=-1, keepdims=True)
    y = (y - mean) * lax.rsqrt(var + LNX_EPS)
    y = y * lnx_w.reshape(H, N).astype(f32) + lnx_b.reshape(H, N).astype(f32)
    y = y + jnp.sum(rh * kh * r_k.astype(f32), axis=-1, keepdims=True) * vh
    y = y.reshape(B, T, D).astype(h.dtype)
    return (y * jax.nn.silu(z)) @ w_out


def setup_inputs(seed: int = 0) -> dict:
    key = jax.random.key(seed)
    ks = iter(jax.random.split(key, 40))
    D, H = D_MODEL, FOX_HEADS

    def nrm(shape, scale):
        return jax.random.normal(next(ks), shape, jnp.float32) * scale

    def uni(shape, lo, hi):
        return jax.random.uniform(next(ks), shape, jnp.float32, minval=lo, maxval=hi)

    return {
        "x": nrm((BATCH, SEQ, D), 1.0),
        "c": nrm((BATCH, D), 1.0),
        "norm_w": 1.0 + nrm((DEPTH, D), 0.02),
        "w_mod": nrm((DEPTH, D, 3 * D), D ** -0.5),
        "b_mod": nrm((DEPTH, 3 * D), 0.02),
        "fox_w_in": nrm((N_FOX, D, 4 * D + H), D ** -0.5),
        "fox_b_f": uni((N_FOX, H), 1.0, 4.0),
        "fox_q_norm_w": 1.0 + nrm((N_FOX, FOX_HEAD_DIM), 0.02),
        "fox_k_norm_w": 1.0 + nrm((N_FOX, FOX_HEAD_DIM), 0.02),
        "fox_w_out": nrm((N_FOX, D, D), D ** -0.5),
        "rwkv_mu": uni((N_RWKV, N_SHIFT_MIX, D), 0.0, 1.0),
        "rwkv_w_in": nrm((N_RWKV, 4, D, D), D ** -0.5),
        "rwkv_w0": uni((N_RWKV, D), -6.0, -1.0),
        "rwkv_w1": nrm((N_RWKV, D, DECAY_LORA), D ** -0.5),
        "rwkv_w2": nrm((N_RWKV, DECAY_LORA, D), 0.5 * DECAY_LORA ** -0.5),
        "rwkv_a0": nrm((N_RWKV, D), 0.1),
        "rwkv_a1": nrm((N_RWKV, D, ICLR_LORA), D ** -0.5),
        "rwkv_a2": nrm((N_RWKV, ICLR_LORA, D), 0.5 * ICLR_LORA ** -0.5),
        "rwkv_v0": nrm((N_RWKV, D), 0.1),
        "rwkv_v1": nrm((N_RWKV, D, VRES_LORA), D ** -0.5),
        "rwkv_v2": nrm((N_RWKV, VRES_LORA, D), 0.5 * VRES_LORA ** -0.5),
        "rwkv_k_k": 0.85 + nrm((N_RWKV, D), 0.02),
        "rwkv_k_a": 1.0 + nrm((N_RWKV, D), 0.02),
        "rwkv_r_k": nrm((N_RWKV, RWKV_HEADS, RWKV_HEAD_DIM), 0.1),
        "rwkv_lnx_w": 1.0 + nrm((N_RWKV, D), 0.02),
        "rwkv_lnx_b": nrm((N_RWKV, D), 0.02),
        "rwkv_w_out": nrm((N_RWKV, D, D), D ** -0.5),
    }


def reference(x, c, norm_w, w_mod, b_mod, fox_w_in, fox_b_f, fox_q_norm_w, fox_k_norm_w,
              fox_w_out, rwkv_mu, rwkv_w_in, rwkv_w0, rwkv_w1, rwkv_w2, rwkv_a0, rwkv_a1,
              rwkv_a2, rwkv_v0, rwkv_v1, rwkv_v2, rwkv_k_k, rwkv_k_a, rwkv_r_k, rwkv_lnx_w,
              rwkv_lnx_b, rwkv_w_out):
    c_act = jax.nn.silu(c)
    v_first = None
    for i in range(DEPTH):
        mod = c_act @ w_mod[i] + b_mod[i]
        shift, scale, gate = jnp.split(mod, 3, axis=-1)
        h = rms_norm(x, norm_w[i]) * (1.0 + scale[:, None, :]) + shift[:, None, :]
        j = i // N_MIXERS
        if i % N_MIXERS == 0:
            out, v = fox_branch(h, fox_w_in[j], fox_b_f[j], fox_q_norm_w[j],
                                fox_k_norm_w[j], fox_w_out[j])
            if v_first is None:
                v_first = v
        else:
            out = rwkv7_branch(h, v_first, rwkv_mu[j], rwkv_w_in[j], rwkv_w0[j], rwkv_w1[j],
                               rwkv_w2[j], rwkv_a0[j], rwkv_a1[j], rwkv_a2[j], rwkv_v0[j],
                               rwkv_v1[j], rwkv_v2[j], rwkv_k_k[j], rwkv_k_a[j], rwkv_r_k[j],
                               rwkv_lnx_w[j], rwkv_lnx_b[j], rwkv_w_out[j])
        x = x + gate[:, None, :] * out
    return x
```

```python
import contextlib
import numpy as np
import concourse.bass as bass
import concourse.mybir as mybir
from concourse.bass_utils import run_bass_kernel_spmd

F32 = mybir.dt.float32
BF16 = mybir.dt.bfloat16
AF = mybir.ActivationFunctionType
ALU = mybir.AluOpType
AX = mybir.AxisListType

D = 2048
KC = 16
NORM_EPS = 1e-6
LNX_EPS = 64e-5
NEG = -30000.0
STOP = ""
SKIP = set()


class Buf:
    def __init__(self, name, t=None):
        self.name = name
        self.t = t
        self.w = None
        self.r = {}

    def __getitem__(self, k):
        return self.t[k]


class Ctx:
    ENGS = ("pe", "act", "dve", "pool", "sp")

    def __init__(self, nc, stack):
        self.nc = nc
        self.stack = stack
        self.e = {"pe": nc.tensor, "act": nc.scalar, "dve": nc.vector,
                  "pool": nc.gpsimd, "sp": nc.sync}
        self.sems = {}
        self.cnt = {}
        self.seen = {k: {} for k in self.ENGS}
        self.nsem = 0
        self.uid = 0
        self.epoch = {}
        self.free = []
        for k in self.ENGS:
            self._newsem(k)

    def _newsem(self, key):
        h = self.stack.enter_context(self.nc.semaphore("s%d" % self.nsem))
        self.nsem += 1
        self.sems[key] = h
        self.cnt[key] = 0
        return h

    def sb(self, name, shape, dtype=F32, stack=None):
        self.uid += 1
        nm = "%s_%d" % (name, self.uid)
        t = (stack or self.stack).enter_context(self.nc.sbuf_tensor(nm, list(shape), dtype))
        return Buf(nm, t)

    def ps(self, name, shape, dtype=F32, stack=None):
        self.uid += 1
        nm = "%s_%d" % (name, self.uid)
        t = (stack or self.stack).enter_context(self.nc.psum_tensor(nm, list(shape), dtype))
        return Buf(nm, t)

    def _wait(self, eng, ev):
        if ev is None:
            return
        key, val = ev
        if self.seen[eng].get(key, 0) >= val:
            return
        self.e[eng].wait_ge(self.sems[key], val)
        self.seen[eng][key] = val

    def _deps(self, eng, reads, writes):
        for b in reads:
            if b.w is not None and not (eng == "pe" and b.w[0] == "pe"):
                self._wait(eng, b.w)
        for b in writes:
            if b.w is not None and not (eng == "pe" and b.w[0] == "pe"):
                self._wait(eng, b.w)
            for key, val in b.r.items():
                if key == eng and eng == "pe":
                    continue
                self._wait(eng, (key, val))

    def _mark(self, ev, reads, writes):
        key, val = ev
        for b in reads:
            if b.r.get(key, 0) < val:
                b.r[key] = val
        for b in writes:
            b.w = ev
            b.r = {}

    def op(self, eng, fn, reads=(), writes=()):
        self._deps(eng, reads, writes)
        inst = fn(self.e[eng])
        self.cnt[eng] += 1
        inst.then_inc(self.sems[eng], 1)
        ev = (eng, self.cnt[eng])
        self._mark(ev, reads, writes)
        return ev

    def dma(self, q, out_ap, in_ap, reads=(), writes=(), **kw):
        self._deps(q, reads, writes)
        owner = (list(writes) + list(reads))[0]
        bkey = ("dma", owner.name, q)
        key = self.epoch.get(bkey)
        if key is None:
            if self.free:
                key = self.free.pop()
            else:
                key = ("dsem", self.nsem)
                self._newsem(key)
            self.epoch[bkey] = key
        inst = self.e[q].dma_start(out=out_ap, in_=in_ap, **kw)
        self.cnt[key] += 16
        inst.then_inc(self.sems[key], 16)
        ev = (key, self.cnt[key])
        self._mark(ev, reads, writes)
        return ev

    def barrier(self):
        for key, val in self.cnt.items():
            if val > 0:
                self._wait("sp", (key, val))
        self.nc.all_engine_barrier()
        for k in self.ENGS:
            for key, val in self.cnt.items():
                self.seen[k][key] = val
        self.free.extend(self.epoch.values())
        self.epoch.clear()

    def finish(self):
        for key, val in self.cnt.items():
            if val > 0:
                self._wait("sp", (key, val))


class TokWhole:
    def __init__(self, ap):
        self.ap = ap

    def sl(self, rows, t0, n):
        return self.ap[rows, t0:t0 + n]


class TokPieces:
    def __init__(self, aps):
        self.aps = aps

    def sl(self, rows, t0, n):
        p, o = divmod(t0, 512)
        assert o + n <= 512
        return self.aps[p][rows, o:o + n]


def bcast_rows(ap_row):
    return ap_row.partition_broadcast(128)


def phase_mod(c, ca_d, wmod_d, bmod_d, dst, pieces):
    with contextlib.ExitStack() as ph:
        ca = c.sb("ca", [128, KC], stack=ph)
        sg = c.sb("sg", [128, KC], stack=ph)
        ca_rep = c.sb("carep", [128, KC, 128], stack=ph)
        bm = c.sb("bm", [128, 6144], stack=ph)
        wp = [c.sb("wp%d" % i, [128, KC, 512], stack=ph) for i in range(2)]
        pm = [c.ps("pm%d" % i, [128, 512], stack=ph) for i in range(2)]
        c.dma("sp", ca[:], ca_d, writes=[ca])
        c.dma("sp", bm[:].unsqueeze(1), bcast_rows(bmod_d), writes=[bm])
        c.op("act", lambda e: e.activation(out=sg[:], in_=ca[:], func=AF.Sigmoid), reads=[ca], writes=[sg])
        c.op("dve", lambda e: e.tensor_tensor(out=sg[:], in0=sg[:], in1=ca[:], op=ALU.mult), reads=[sg, ca], writes=[sg])
        c.op("dve", lambda e: e.tensor_copy(out=ca_rep[:], in_=sg[:].unsqueeze(2).to_broadcast([128, KC, 128])),
             reads=[sg], writes=[ca_rep])
        for i, pc in enumerate(pieces):
            w = wp[i % 2]
            p = pm[i % 2]
            c.dma("sp", w[:], wmod_d(pc) if callable(wmod_d) else wmod_d[pc], writes=[w])
            for kc in range(KC):
                c.op("pe", lambda e: e.matmul(p[:], ca_rep[:, kc, :], w[:, kc, :], start=(kc == 0), stop=(kc == KC - 1)),
                     reads=[ca_rep, w], writes=[p])
            db, c0 = dst(pc)
            c.op("dve", lambda e: e.tensor_tensor(out=db[:, c0:c0 + 512], in0=p[:],
                                                  in1=bm[:, pc * 512:(pc + 1) * 512], op=ALU.add),
                 reads=[p, bm], writes=[db])
        c.barrier()


def make_A(c, normw_d, A_bc):
    with contextlib.ExitStack() as ph:
        nw = c.sb("nw", [128, D], stack=ph)
        c.dma("sp", nw[:].unsqueeze(1), bcast_rows(normw_d), writes=[nw])
        c.op("dve", lambda e: e.scalar_tensor_tensor(out=A_bc[:], in0=A_bc[:], scalar=1.0, in1=nw[:],
                                                     op0=ALU.add, op1=ALU.mult), reads=[A_bc, nw], writes=[A_bc])
        c.barrier()


def norm_tile(c, x_ap, xt, ss, sd, rstd, hm, A_bc, shift_bc):
    c.dma("sp", xt[:], x_ap, writes=[xt])
    c.op("act", lambda e: e.activation(out=hm[:], in_=xt[:], func=AF.Square, accum_out=ss[:]), reads=[xt], writes=[hm, ss])
    c.op("act", lambda e: e.activation(out=sd[:], in_=ss[:], func=AF.Sqrt, scale=1.0 / D, bias=NORM_EPS), reads=[ss], writes=[sd])
    c.op("dve", lambda e: e.reciprocal(out=rstd[:], in_=sd[:]), reads=[sd], writes=[rstd])
    c.op("dve", lambda e: e.scalar_tensor_tensor(out=hm[:], in0=xt[:], scalar=rstd[:, 0:1], in1=A_bc[:],
                                                 op0=ALU.mult, op1=ALU.mult), reads=[xt, rstd, A_bc], writes=[hm])
    c.op("pool", lambda e: e.tensor_tensor(out=hm[:], in0=hm[:], in1=shift_bc[:], op=ALU.add), reads=[hm, shift_bc], writes=[hm])


def transpose_tile(c, hm, hT, col0, ident, ptr, cnt):
    for g4 in range(4):
        p = ptr[cnt[0] % 2]
        cnt[0] += 1
        for j in range(4):
            kc = g4 * 4 + j
            c.op("pe", lambda e: e.transpose(out=p[:, j, :], in_=hm[:, kc * 128:(kc + 1) * 128], identity=ident[:]),
                 reads=[hm, ident], writes=[p])
        if g4 % 2 == 0:
            c.op("act", lambda e: e.activation(out=hT[:, g4 * 4:(g4 + 1) * 4, col0:col0 + 128], in_=p[:], func=AF.Copy),
                 reads=[p], writes=[hT])
        else:
            c.op("dve", lambda e: e.tensor_copy(out=hT[:, g4 * 4:(g4 + 1) * 4, col0:col0 + 128], in_=p[:]),
                 reads=[p], writes=[hT])


def build_A(T, fused=None):
    nc = fused["nc"] if fused else bass.Bass("TRN2", target_bir_lowering=False)
    NB = T // 128
    CH = 256
    NCH = T // CH
    QC = 512
    NQ = T // QC

    def din(name, shape):
        if fused:
            if name in fused["share"]:
                return fused["share"][name]
            return nc.dram_tensor("A_" + name, list(shape), F32, kind="ExternalInput").ap()
        return nc.dram_tensor(name, list(shape), F32, kind="ExternalInput").ap()

    x_d = din("x", [T, D])
    ca_d = din("ca", [128, KC])
    wmod_d = din("wmod", [12, 128, KC, 512])
    bmod_d = din("bmod", [1, 6144])
    normw_d = din("normw", [1, D])
    wqkz_d = din("wqkz", [12, 128, KC, 128])
    wv_d = din("wv", [128, KC, 512])
    wf_d = din("wf", [128, KC, 8])
    bf_d = din("bf", [8, 1])
    qkw_d = din("qkw", [128, 2])
    ident_d = din("ident", [128, 128])
    bones_d = din("bones", [128, 128])
    masks_d = din("masks", [128, 4, 512])
    if fused:
        ogt_d = fused["share"]["A_ogt"]
        v_d = fused["share"]["A_v"]
    else:
        ogt_d = TokWhole(nc.dram_tensor("ogt", [512, T], F32, kind="ExternalOutput").ap())
        v_d = nc.dram_tensor("v", [T, 512], F32, kind="ExternalOutput").ap()
    qt_d = nc.dram_tensor("qt_s", [512, T], F32, kind="Internal").ap()
    kt_d = nc.dram_tensor("kt_s", [512, T], F32, kind="Internal").ap()
    zt_d = nc.dram_tensor("zt_s", [512, T], F32, kind="Internal").ap()
    cum_d = nc.dram_tensor("cum_s", [8, T], F32, kind="Internal").ap()

    with contextlib.ExitStack() as st:
        c = fused["c"] if fused else Ctx(nc, st)
        ident = c.sb("ident", [128, 128], stack=st)
        bones = c.sb("bones", [128, 128], stack=st)
        ncT = c.sb("ncT", [128, NB, 8], stack=st)
        c.dma("sp", ident[:], ident_d, writes=[ident])
        c.dma("sp", bones[:], bones_d, writes=[bones])
        cst = c.sb("cst", [128, 4], stack=st)
        c.op("dve", lambda e: e.memset(cst[:, 0:1], NORM_EPS), writes=[cst])
        c.op("dve", lambda e: e.memset(cst[:, 1:2], 64.0 * NORM_EPS), writes=[cst])
        c.op("dve", lambda e: e.memset(cst[:, 2:3], 1.0), writes=[cst])
        with contextlib.ExitStack() as g1:
            shift_bc = c.sb("shiftbc", [128, D], stack=g1)
            A_bc = c.sb("Abc", [128, D], stack=g1)
            phase_mod(c, ca_d, wmod_d, bmod_d,
                      lambda pc: (shift_bc, pc * 512) if pc < 4 else (A_bc, (pc - 4) * 512), list(range(8)))
            make_A(c, normw_d, A_bc)
            if STOP == "p0":
                c.op("dve", lambda e: e.tensor_copy(out=ident[:], in_=shift_bc[:, 0:128]), reads=[shift_bc], writes=[ident])
                c.dma("sp", v_d[0:128, 0:128], ident[:], reads=[ident])
                c.op("dve", lambda e: e.tensor_copy(out=bones[:], in_=A_bc[:, 0:128]), reads=[A_bc], writes=[bones])
                c.dma("sp", v_d[0:128, 128:256], bones[:], reads=[bones])
                c.finish()
                return nc

            for sweep in range(2):
                with contextlib.ExitStack() as ph:
                    noc = 8 if sweep == 0 else 4
                    wq = c.sb("wq", [128, noc, KC, 128], stack=ph)
                    hT = c.sb("hT", [128, KC, CH], stack=ph)
                    xts = [c.sb("xt%d" % i, [128, D], stack=ph) for i in range(2)]
                    hm = c.sb("hm", [128, D], stack=ph)
                    sss = [c.sb("ss%d" % i, [128, 1], stack=ph) for i in range(2)]
                    sd = c.sb("sd", [128, 1], stack=ph)
                    rstd = c.sb("rstd", [128, 1], stack=ph)
                    qn = [c.sb("qn%d" % i, [128, CH], stack=ph) for i in range(3)]
                    ptr = [c.ps("ptr%d" % i, [128, 4, 128], stack=ph) for i in range(2)]
                    pq = [c.ps("pq%d" % i, [128, CH], stack=ph) for i in range(2)]
                    for oc in range(noc):
                        c.dma("sp", wq[:, oc, :, :], wqkz_d[oc + (0 if sweep == 0 else 8)], writes=[wq])
                    if sweep == 0:
                        wf = c.sb("wf", [128, KC, 8], stack=ph)
                        bf = c.sb("bf", [8, 1], stack=ph)
                        nbf = c.sb("nbf", [8, 1], stack=ph)
                        qkw = c.sb("qkw", [128, 2], stack=ph)
                        ones8 = c.sb("ones8", [8, CH], stack=ph)
                        sq = [c.sb("sq%d" % i, [128, CH], stack=ph) for i in range(2)]
                        qraw = [c.sb("qraw%d" % i, [128, CH], stack=ph) for i in range(2)]
                        sd2 = [c.sb("sd2%d" % i, [128, CH], stack=ph) for i in range(2)]
                        fe = c.sb("fe", [8, CH], stack=ph)
                        cums = [c.sb("cum%d" % i, [8, CH], stack=ph) for i in range(2)]
                        pss = [c.ps("pss%d" % i, [128, CH], stack=ph) for i in range(2)]
                        pf = c.ps("pf", [128, 512], stack=ph)
                        c.dma("sp", wf[:], wf_d, writes=[wf])
                        c.dma("sp", bf[:], bf_d, writes=[bf])
                        c.dma("sp", qkw[:], qkw_d, writes=[qkw])
                        c.op("dve", lambda e: e.tensor_scalar(out=nbf[:], in0=bf[:], scalar1=-1.0, scalar2=None, op0=ALU.mult), reads=[bf], writes=[nbf])
                        c.op("dve", lambda e: e.memset(ones8[:], 1.0), writes=[ones8])
                    else:
                        wv = c.sb("wv", [128, KC, 512], stack=ph)
                        vt = [c.sb("vt%d" % i, [128, 512], stack=ph) for i in range(2)]
                        pv = [c.ps("pv%d" % i, [128, 512], stack=ph) for i in range(2)]
                        c.dma("sp", wv[:], wv_d, writes=[wv])
                    tcnt = [0]
                    it = 0
                    for ch in range(NCH):
                        for tt in range(2):
                            tok0 = ch * CH + tt * 128
                            norm_tile(c, x_d[tok0:tok0 + 128, :], xts[it % 2], sss[it % 2], sd, rstd, hm, A_bc, shift_bc)
                            transpose_tile(c, hm, hT, tt * 128, ident, ptr, tcnt)
                            it += 1
                        tsl = slice(ch * CH, (ch + 1) * CH)
                        for oc in range(noc):
                            p = pq[oc % 2]
                            for kc in range(KC):
                                c.op("pe", lambda e: e.matmul(p[:], wq[:, oc, kc, :], hT[:, kc, :], start=(kc == 0), stop=(kc == KC - 1)),
                                     reads=[wq, hT], writes=[p])
                            o = qn[oc % 3]
                            if sweep == 0 and "qk" not in SKIP:
                                s_ = sq[oc % 2]; qr = qraw[oc % 2]; s2 = sd2[oc % 2]; pz = pss[oc % 2]
                                c.op("act", lambda e: e.activation(out=s_[:], in_=p[:], func=AF.Square), reads=[p], writes=[s_])
                                c.op("act", lambda e: e.activation(out=qr[:], in_=p[:], func=AF.Copy), reads=[p], writes=[qr])
                                c.op("pe", lambda e: e.matmul(pz[:], bones[:], s_[:], start=True, stop=True), reads=[bones, s_], writes=[pz])
                                if oc < 4:
                                    c.op("act", lambda e: e.activation(out=s2[:], in_=pz[:], func=AF.Sqrt, scale=1.0, bias=cst[:, 1:2]), reads=[pz, cst], writes=[s2])
                                else:
                                    c.op("act", lambda e: e.activation(out=s2[:], in_=pz[:], func=AF.Sqrt, scale=1.0 / 64.0, bias=cst[:, 0:1]), reads=[pz, cst], writes=[s2])
                                c.op("dve", lambda e: e.reciprocal(out=s2[:], in_=s2[:]), reads=[s2], writes=[s2])
                                wcol = 0 if oc < 4 else 1
                                c.op("dve", lambda e: e.scalar_tensor_tensor(out=o[:], in0=qr[:], scalar=qkw[:, wcol:wcol + 1], in1=s2[:],
                                                                             op0=ALU.mult, op1=ALU.mult), reads=[qr, qkw, s2], writes=[o])
                                dst = qt_d if oc < 4 else kt_d
                            else:
                                c.op("act", lambda e: e.activation(out=o[:], in_=p[:], func=AF.Copy), reads=[p], writes=[o])
                                dst = zt_d if sweep == 1 else (qt_d if oc < 4 else kt_d)
                            r0 = (oc % 4) * 128
                            c.dma("sp", dst[r0:r0 + 128, tsl], o[:], reads=[o])
                        if sweep == 1:
                            for tt in range(2):
                                p = pv[tt % 2]
                                for kc in range(KC):
                                    c.op("pe", lambda e: e.matmul(p[:], hT[:, kc, tt * 128:(tt + 1) * 128], wv[:, kc, :], start=(kc == 0), stop=(kc == KC - 1)),
                                         reads=[hT, wv], writes=[p])
                                o = vt[tt % 2]
                                c.op("act", lambda e: e.activation(out=o[:], in_=p[:], func=AF.Copy), reads=[p], writes=[o])
                                tok0 = ch * CH + tt * 128
                                c.dma("sp", v_d[tok0:tok0 + 128, :], o[:], reads=[o])
                        elif "f" not in SKIP:
                            for kc in range(KC):
                                c.op("pe", lambda e: e.matmul(pf[0:8, 0:CH], wf[:, kc, :], hT[:, kc, :], start=(kc == 0), stop=(kc == KC - 1)),
                                     reads=[wf, hT], writes=[pf])
                            c.op("act", lambda e: e.activation(out=fe[:], in_=pf[0:8, 0:CH], func=AF.Exp, scale=-1.0, bias=nbf[:, 0:1]), reads=[pf, nbf], writes=[fe])
                            c.op("act", lambda e: e.activation(out=fe[:], in_=fe[:], func=AF.Ln, scale=1.0, bias=cst[0:8, 2:3]), reads=[fe, cst], writes=[fe])
                            cu = cums[ch % 2]
                            prev = cums[(ch + 1) % 2]
                            init = 0.0 if ch == 0 else prev[:, CH - 1:CH]
                            c.op("dve", lambda e: e.tensor_tensor_scan(out=cu[:], data0=ones8[:], data1=fe[:], initial=init,
                                                                       op0=ALU.mult, op1=ALU.subtract), reads=[ones8, fe, prev], writes=[cu])
                            c.dma("sp", cum_d[:, tsl], cu[:], reads=[cu])
                            for tt in range(2):
                                c.op("pe", lambda e: e.transpose(out=pf[:, 256 + tt * 8:256 + tt * 8 + 8], in_=cu[:, tt * 128:(tt + 1) * 128], identity=ident[0:8, 0:8]),
                                     reads=[cu, ident], writes=[pf])
                            c.op("dve", lambda e: e.tensor_scalar(out=ncT[:, ch * 2:ch * 2 + 2, :], in0=pf[:, 256:272].rearrange("p (a b) -> p a b", a=2),
                                                                  scalar1=-1.0, scalar2=None, op0=ALU.mult), reads=[pf], writes=[ncT])
                    c.barrier()

        if STOP == "p1":
            c.finish()
            return nc
        with contextlib.ExitStack() as ph:
            KT = c.sb("KT", [128, T], stack=ph)
            VA = c.sb("VA", [128, NB, 128], stack=ph)
            masks = c.sb("masks", [128, 4, 512], stack=ph)
            qts = [c.sb("qt%d" % i, [128, QC], stack=ph) for i in range(2)]
            cts = [c.sb("ct%d" % i, [128, QC], stack=ph) for i in range(2)]
            ctms = [c.sb("ctm%d" % i, [128, 4, QC], stack=ph) for i in range(1)]
            zts = [c.sb("zt%d" % i, [64, QC], stack=ph) for i in range(2)]
            tmps = [c.sb("tmp%d" % i, [128, QC], stack=ph) for i in range(3)]
            pTs = [c.sb("pT%d" % i, [128, QC], stack=ph) for i in range(3)]
            rd = c.sb("rd", [128, QC], stack=ph)
            ot = c.sb("ot", [64, QC], stack=ph)
            ez = c.sb("ez", [64, QC], stack=ph)
            ogs = [c.sb("og%d" % i, [64, QC], stack=ph) for i in range(2)]
            psS = [c.ps("psS%d" % i, [128, QC], stack=ph) for i in range(3)]
            pos = [c.ps("po%d" % i, [128, QC], stack=ph) for i in range(2)]
            c.dma("sp", masks[:], masks_d, writes=[masks])
            c.op("pool", lambda e: e.memset(VA[:, :, 64:128], 1.0), writes=[VA])
            pi = 0
            ci = 0
            for hp in range(4):
                nsp = max(1, T // 2048)
                for s in range(nsp):
                    w_ = T // nsp
                    c.dma("sp", KT[:, s * w_:(s + 1) * w_], kt_d[hp * 128:(hp + 1) * 128, s * w_:(s + 1) * w_], writes=[KT])
                for par in range(2):
                    h = hp * 2 + par
                    rows = slice(par * 64, par * 64 + 64)
                    vsrc = v_d[:, h * 64:(h + 1) * 64].rearrange("(b p) d -> p b d", p=128)
                    nvs = max(1, NB // 16)
                    for s in range(nvs):
                        b0 = s * (NB // nvs); b1 = (s + 1) * (NB // nvs)
                        c.dma("sp", VA[:, b0:b1, 0:64], vsrc[:, b0:b1, :], writes=[VA])
                    for cq in range(NQ):
                        qsl = slice(cq * QC, (cq + 1) * QC)
                        qt = qts[ci % 2]; ct = cts[ci % 2]; ctm = ctms[0]; zt = zts[ci % 2]
                        po = pos[ci % 2]; og = ogs[ci % 2]
                        ci += 1
                        c.dma("sp", qt[rows, :], qt_d[h * 64:(h + 1) * 64, qsl], writes=[qt])
                        c.dma("sp", ct[:].unsqueeze(1), bcast_rows(cum_d[h:h + 1, qsl]), writes=[ct])
                        c.dma("sp", zt[:], zt_d[h * 64:(h + 1) * 64, qsl], writes=[zt])
                        c.op("pool", lambda e: e.tensor_tensor(out=ctm[:], in0=masks[:], in1=ct[:].unsqueeze(1).to_broadcast([128, 4, QC]), op=ALU.add),
                             reads=[masks, ct], writes=[ctm])
                        nj = 4 * cq + 4
                        for j in range(nj):
                            ps_ = psS[pi % 3]; tmp = tmps[pi % 3]; pT = pTs[pi % 3]
                            pi += 1
                            c.op("pe", lambda e: e.matmul(ps_[:], KT[rows, j * 128:(j + 1) * 128], qt[rows, :], start=True, stop=True),
                                 reads=[KT, qt], writes=[ps_])
                            jj = j - 4 * cq
                            if jj < 0:
                                c.op("dve", lambda e: e.tensor_tensor(out=tmp[:], in0=ps_[:], in1=ct[:], op=ALU.add), reads=[ps_, ct], writes=[tmp])
                            else:
                                c.op("dve", lambda e: e.tensor_tensor(out=tmp[:], in0=ps_[:], in1=ctm[:, jj, :], op=ALU.add), reads=[ps_, ctm], writes=[tmp])
                            c.op("act", lambda e: e.activation(out=pT[:], in_=tmp[:], func=AF.Exp, scale=1.0, bias=ncT[:, j, h:h + 1]),
                                 reads=[tmp, ncT], writes=[pT])
                            c.op("pe", lambda e: e.matmul(po[:], VA[:, j, :], pT[:], start=(j == 0), stop=(j == nj - 1)),
                                 reads=[VA, pT], writes=[po])
                        c.op("dve", lambda e: e.reciprocal(out=rd[64:128, :], in_=po[64:128, :]), reads=[po], writes=[rd])
                        c.op("dve", lambda e: e.tensor_tensor(out=ot[:], in0=po[0:64, :], in1=rd[64:128, :], op=ALU.mult), reads=[po, rd], writes=[ot])
                        c.op("act", lambda e: e.activation(out=ez[:], in_=zt[:], func=AF.Exp, scale=-1.0), reads=[zt], writes=[ez])
                        c.op("pool", lambda e: e.tensor_scalar(out=ez[:], in0=ez[:], scalar1=1.0, scalar2=None, op0=ALU.add), reads=[ez], writes=[ez])
                        c.op("dve", lambda e: e.reciprocal(out=ez[:], in_=ez[:]), reads=[ez], writes=[ez])
                        c.op("pool", lambda e: e.tensor_tensor(out=ez[:], in0=ez[:], in1=zt[:], op=ALU.mult), reads=[ez, zt], writes=[ez])
                        c.op("dve", lambda e: e.tensor_tensor(out=og[:], in0=ot[:], in1=ez[:], op=ALU.mult), reads=[ot, ez], writes=[og])
                        c.dma("sp", ogt_d.sl(slice(h * 64, (h + 1) * 64), cq * QC, QC), og[:], reads=[og])
            c.barrier()
        if not fused:
            c.finish()
    return nc


def consts():
    ident = np.eye(128, dtype=np.float32)
    bones = np.zeros((128, 128), np.float32)
    bones[:64, :64] = 1.0
    bones[64:, 64:] = 1.0
    p = np.arange(128)[:, None]
    col = np.arange(512)[None, :]
    masks = np.zeros((128, 4, 512), np.float32)
    for jj in range(4):
        masks[:, jj, :] = np.where(col >= 128 * jj + p, 0.0, NEG)
    return ident, bones, masks


def fm(vec):
    return np.ascontiguousarray(vec.reshape(KC, 128).T)


def wpieces(w, ncols):
    n = w.shape[1] // ncols
    return np.ascontiguousarray(w.reshape(KC, 128, n, ncols).transpose(2, 1, 0, 3))


def inputs_A(inp, T):
    ident, bones, masks = consts()
    maps = []
    wmod = wpieces(inp["w_mod"][0], 512)
    win = inp["fox_w_in"][0]
    for core in range(8):
        b, g = core // 4, core % 4
        cols = []
        for typ in (0, 1, 3):
            for oc in range(4):
                c0 = typ * D + g * 512 + oc * 128
                cols.append(win[:, c0:c0 + 128])
        wqkz = np.ascontiguousarray(np.stack(cols, 0).reshape(12, KC, 128, 128).transpose(0, 2, 1, 3))
        wv = np.ascontiguousarray(win[:, 2 * D + g * 512:2 * D + (g + 1) * 512].reshape(KC, 128, 512).transpose(1, 0, 2))
        wf = np.ascontiguousarray(win[:, 4 * D + g * 8:4 * D + (g + 1) * 8].reshape(KC, 128, 8).transpose(1, 0, 2))
        qkw = np.stack([np.tile(inp["fox_q_norm_w"][0], 2), np.tile(inp["fox_k_norm_w"][0], 2)], 1).astype(np.float32)
        maps.append({
            "x": np.ascontiguousarray(inp["x"][b, :T]),
            "ca": fm(inp["c"][b]),
            "wmod": wmod,
            "bmod": np.ascontiguousarray(inp["b_mod"][0][None, :]),
            "normw": np.ascontiguousarray(inp["norm_w"][0][None, :]),
            "wqkz": wqkz, "wv": wv, "wf": wf,
            "bf": np.ascontiguousarray(inp["fox_b_f"][0][g * 8:(g + 1) * 8, None]),
            "qkw": np.ascontiguousarray(qkw),
            "ident": ident, "bones": bones, "masks": masks,
        })
    return maps


def run_A(inp, T):
    nc = build_A(T)
    res = run_bass_kernel_spmd(nc, inputs_A(inp, T), core_ids=list(range(8)))
    return res.results


def phase_resid(c, ntiles, xsrc, asrc, wout_d, gate_bc, store):
    with contextlib.ExitStack() as ph:
        wo = c.sb("wo", [128, KC, D], stack=ph)
        for k4 in range(4):
            c.dma("sp", wo[:, k4 * 4:(k4 + 1) * 4, :], wout_d[:, k4 * 4:(k4 + 1) * 4, :], writes=[wo])
        aT = [c.sb("aT%d" % i, [128, KC, 128], stack=ph) for i in range(2)]
        xt = [c.sb("xr%d" % i, [128, D], stack=ph) for i in range(2)]
        ot = [c.sb("or%d" % i, [128, D], stack=ph) for i in range(2)]
        pp = [c.ps("pr%d" % i, [128, 512], stack=ph) for i in range(4)]
        for i in range(ntiles):
            a = aT[i % 2]; x = xt[i % 2]; o = ot[i % 2]
            c.dma("sp", a[:], asrc(i), writes=[a])
            c.dma("sp", x[:], xsrc(i), writes=[x])
            for n in range(4):
                p = pp[n]
                for kc in range(KC):
                    c.op("pe", lambda e: e.matmul(p[:], a[:, kc, :], wo[:, kc, n * 512:(n + 1) * 512], start=(kc == 0), stop=(kc == KC - 1)),
                         reads=[a, wo], writes=[p])
                c.op("dve", lambda e: e.tensor_tensor(out=o[:, n * 512:(n + 1) * 512], in0=p[:], in1=gate_bc[:, n * 512:(n + 1) * 512], op=ALU.mult),
                     reads=[p, gate_bc], writes=[o])
            c.op("pool", lambda e: e.tensor_tensor(out=o[:], in0=o[:], in1=x[:], op=ALU.add), reads=[o, x], writes=[o])
            store(i, o)
        c.barrier()


def fm_rows(tw, tok0, n=128):
    return tw.sl(slice(None), tok0, n).rearrange("(kc p) t -> p kc t", p=128)


WDECAY = -0.6065306597126334


def build_B(T, fused=None):
    nc = fused["nc"] if fused else bass.Bass("TRN2", target_bir_lowering=False)
    NT = T // 128
    TQ = T // 4

    def din(name, shape):
        if fused:
            if name in fused["share"]:
                return fused["share"][name]
            return nc.dram_tensor("B_" + name, list(shape), F32, kind="ExternalInput").ap()
        return nc.dram_tensor(name, list(shape), F32, kind="ExternalInput").ap()

    def dscr(name, shape):
        if fused and name in fused["share"]:
            return fused["share"][name]
        return nc.dram_tensor(name, list(shape), F32, kind="Internal").ap()

    x_d = din("x", [T, D])
    ogt_d = fused["share"]["ogt"] if fused else TokWhole(din("ogt", [D, T]))
    ca_d = din("ca", [128, KC])
    wmod0_d = din("wmod0g", [4, 128, KC, 512])
    bmod0_d = din("bmod0", [1, 6144])
    wmod1_d = din("wmod1", [8, 128, KC, 512])
    bmod1_d = din("bmod1", [1, 6144])
    normw_d = din("normw", [1, D])
    wout_d = din("wout", [128, KC, D])
    mu_d = din("mu", [128, 6, KC])
    w4_d = din("w4", [4, 2, 128, KC, 256])
    l1_d = din("l1", [128, KC, 256])
    w2_d = din("w2", [96, 512])
    a2_d = din("a2", [96, 512])
    v2_d = din("v2", [64, 512])
    pvec_d = din("pvec", [6, 512])
    lnx_d = din("lnx", [128, 2, 4])
    vf_d = din("vfirst", [T, 512])
    ident_d = din("ident", [128, 128])
    bones_d = din("bones", [128, 128])
    sel_d = din("sel", [32, 16, 128])
    if fused:
        ygt_d = fused["share"]["B_ygt"]
    else:
        ygt_d = TokWhole(nc.dram_tensor("ygt", [512, T], F32, kind="ExternalOutput").ap())
    X1 = dscr("x1_s", [T, D])
    R5 = dscr("r5_s", [T, 2, 5, 4, 64])
    VT = dscr("vt_s", [512, T])
    BVT = dscr("bvt_s", [512, T])
    SZT = dscr("szt_s", [512, T])
    YT = dscr("yt_s", [512, T])

    with contextlib.ExitStack() as st:
        c = fused["c"] if fused else Ctx(nc, st)
        ident = c.sb("ident", [128, 128], stack=st)
        bones = c.sb("bones", [128, 128], stack=st)
        cst = c.sb("cst", [128, 4], stack=st)
        c.dma("sp", ident[:], ident_d, writes=[ident])
        c.dma("sp", bones[:], bones_d, writes=[bones])
        c.op("dve", lambda e: e.memset(cst[:, 0:1], NORM_EPS), writes=[cst])
        c.op("dve", lambda e: e.memset(cst[:, 1:2], LNX_EPS), writes=[cst])
        c.op("dve", lambda e: e.memset(cst[:, 2:3], 0.0), writes=[cst])

        if "b1" not in SKIP:
            with contextlib.ExitStack() as g0:
                gate_bc = c.sb("gate0", [128, D], stack=g0)
                phase_mod(c, ca_d, lambda pc: wmod0_d[pc - 8], bmod0_d, lambda pc: (gate_bc, (pc - 8) * 512), [8, 9, 10, 11])
                phase_resid(c, NT, lambda i: x_d[i * 128:(i + 1) * 128, :], lambda i: fm_rows(ogt_d, i * 128), wout_d, gate_bc,
                            lambda i, o: c.dma("sp", X1[i * 128:(i + 1) * 128, :], o[:], reads=[o]))
        if STOP == "b1":
            c.finish()
            return nc

        with contextlib.ExitStack() as g1:
            shift_bc = c.sb("shift1", [128, D], stack=g1)
            A_bc = c.sb("A1", [128, D], stack=g1)
            phase_mod(c, ca_d, wmod1_d, bmod1_d,
                      lambda pc: (shift_bc, pc * 512) if pc < 4 else (A_bc, (pc - 4) * 512), list(range(8)))
            make_A(c, normw_d, A_bc)
            with contextlib.ExitStack() as ph:
                def T2(name):
                    return c.sb(name, [128, 512], stack=ph)
                mu = c.sb("mu", [128, 6, KC], stack=ph)
                l1 = c.sb("l1", [128, KC, 256], stack=ph)
                w2 = c.sb("w2", [96, 512], stack=ph)
                a2 = c.sb("a2", [96, 512], stack=ph)
                v2 = c.sb("v2", [64, 512], stack=ph)
                pv_ = [T2("pvec%d" % i) for i in range(6)]
                omk = T2("omk")
                wb = [c.sb("wb%d" % i, [128, KC, 256], stack=ph) for i in range(3)]
                xt = c.sb("xt", [128, D], stack=ph)
                hm = c.sb("hm", [128, D], stack=ph)
                ss = c.sb("ss", [128, 1], stack=ph)
                sd = c.sb("sd", [128, 1], stack=ph)
                rstd = c.sb("rstd", [128, 1], stack=ph)
                hT = c.sb("hT", [128, KC, 129], stack=ph)
                xx = c.sb("xx", [128, KC, 128], stack=ph)
                mixs = [c.sb("mix%d" % i, [128, KC, 128], stack=ph) for i in range(2)]
                r_sb = T2("r_sb"); w_sb = T2("w_sb"); k_sb = T2("k_sb"); a_sb = T2("a_sb"); kk = T2("kk")
                av = T2("av"); bv_ = T2("bv"); tq = T2("tq"); v_sb = T2("v_sb"); vg = T2("vg"); vf = T2("vf")
                vT_sb = c.sb("vT_sb", [128, 4, 128], stack=ph)
                bT_sb = c.sb("bT_sb", [128, 4, 128], stack=ph)
                zs = c.sb("zs", [128, 4, 128], stack=ph)
                lo = [c.sb("lo%d" % i, [96, 128], stack=ph) for i in range(3)]
                s8 = c.sb("s8", [128, 8], stack=ph)
                r8 = c.sb("r8", [128, 8], stack=ph)
                bon = c.sb("bon", [128, 8], stack=ph)
                ptr = [c.ps("ptr%d" % i, [128, 4, 128], stack=ph) for i in range(2)]
                pb = [c.ps("pb%d" % i, [128, 512], stack=ph) for i in range(3)]
                pl = c.ps("pl", [128, 512], stack=ph)
                pz = c.ps("pz", [128, 4, 128], stack=ph)
                c.dma("sp", mu[:], mu_d, writes=[mu])
                c.dma("sp", l1[:], l1_d, writes=[l1])
                c.dma("sp", w2[:], w2_d, writes=[w2])
                c.dma("sp", a2[:], a2_d, writes=[a2])
                c.dma("sp", v2[:], v2_d, writes=[v2])
                for i in range(6):
                    c.dma("sp", pv_[i][:].unsqueeze(1), bcast_rows(pvec_d[i:i + 1, :]), writes=[pv_[i]])
                w0b, a0b, v0b, kkb, kab, rkb = pv_
                c.op("dve", lambda e: e.tensor_scalar(out=omk[:], in0=kab[:], scalar1=-1.0, scalar2=1.0, op0=ALU.mult, op1=ALU.add), reads=[kab], writes=[omk])
                c.op("dve", lambda e: e.memset(hT[:, :, 0:1], 0.0), writes=[hT])
                tcnt = [0]
                wi = [0]
                pbi = [0]

                def proj_tm(mix, widx):
                    p = pb[pbi[0] % 3]
                    pbi[0] += 1
                    for half in range(2):
                        w = wb[wi[0] % 3]
                        wi[0] += 1
                        c.dma("sp", w[:], w4_d[widx, half], writes=[w])
                        for kc in range(KC):
                            c.op("pe", lambda e: e.matmul(p[:, half * 256:(half + 1) * 256], mix[:, kc, :], w[:, kc, :], start=(kc == 0), stop=(kc == KC - 1)),
                                 reads=[mix, w], writes=[p])
                    return p

                def lora_tm(mix, c0, n, act, w2t, lot):
                    for kc in range(KC):
                        c.op("pe", lambda e: e.matmul(pl[0:n, 0:128], l1[:, kc, c0:c0 + n], mix[:, kc, :], start=(kc == 0), stop=(kc == KC - 1)),
                             reads=[l1, mix], writes=[pl])
                    c.op("act", lambda e: e.activation(out=lot[0:n, :], in_=pl[0:n, 0:128], func=act), reads=[pl], writes=[lot])
                    p = pb[pbi[0] % 3]
                    pbi[0] += 1
                    c.op("pe", lambda e: e.matmul(p[:], lot[0:n, :], w2t[0:n, :], start=True, stop=True), reads=[lot, w2t], writes=[p])
                    return p

                def store_r5(tile, q, tok0):
                    tv = tile[:].rearrange("p (hg par k) -> p hg par k", hg=4, par=2)
                    for par in range(2):
                        c.dma("sp", R5[tok0:tok0 + 128, par, q, :, :], tv[:, :, par, :], reads=[tile])

                def v3(t_):
                    return t_[:].rearrange("p (h k) -> p h k", h=8)

                for it in range(NT):
                    tok0 = it * 128
                    norm_tile(c, X1[tok0:tok0 + 128, :], xt, ss, sd, rstd, hm, A_bc, shift_bc)
                    for g4 in range(4):
                        p = ptr[tcnt[0] % 2]
                        tcnt[0] += 1
                        for j in range(4):
                            kc = g4 * 4 + j
                            c.op("pe", lambda e: e.transpose(out=p[:, j, :], in_=hm[:, kc * 128:(kc + 1) * 128], identity=ident[:]),
                                 reads=[hm, ident], writes=[p])
                        c.op("act", lambda e: e.activation(out=hT[:, g4 * 4:(g4 + 1) * 4, 1:129], in_=p[:], func=AF.Copy), reads=[p], writes=[hT])
                    c.op("dve", lambda e: e.tensor_tensor(out=xx[:], in0=hT[:, :, 0:128], in1=hT[:, :, 1:129], op=ALU.subtract), reads=[hT], writes=[xx])

                    def mk_mix(q):
                        m = mixs[q % 2]
                        c.op("dve", lambda e: e.tensor_tensor(out=m[:], in0=xx[:], in1=mu[:, q, :].unsqueeze(2).to_broadcast([128, KC, 128]), op=ALU.mult),
                             reads=[xx, mu], writes=[m])
                        c.op("pool", lambda e: e.tensor_tensor(out=m[:], in0=m[:], in1=hT[:, :, 1:129], op=ALU.add), reads=[m, hT], writes=[m])
                        return m

                    m = mk_mix(0)
                    p = proj_tm(m, 0)
                    c.op("act", lambda e: e.activation(out=r_sb[:], in_=p[:], func=AF.Copy), reads=[p], writes=[r_sb])
                    m = mk_mix(1)
                    p = lora_tm(m, 0, 96, AF.Tanh, w2, lo[0])
                    c.op("dve", lambda e: e.tensor_tensor(out=w_sb[:], in0=p[:], in1=w0b[:], op=ALU.add), reads=[p, w0b], writes=[w_sb])
                    c.op("act", lambda e: e.activation(out=w_sb[:], in_=w_sb[:], func=AF.Sigmoid), reads=[w_sb], writes=[w_sb])
                    c.op("act", lambda e: e.activation(out=w_sb[:], in_=w_sb[:], func=AF.Exp, scale=WDECAY), reads=[w_sb], writes=[w_sb])
                    m = mk_mix(2)
                    p = proj_tm(m, 1)
                    c.op("act", lambda e: e.activation(out=k_sb[:], in_=p[:], func=AF.Copy), reads=[p], writes=[k_sb])
                    m = mk_mix(3)
                    p = proj_tm(m, 2)
                    c.op("act", lambda e: e.activation(out=v_sb[:], in_=p[:], func=AF.Copy), reads=[p], writes=[v_sb])
                    p = lora_tm(m, 192, 64, AF.Copy, v2, lo[1])
                    c.op("dve", lambda e: e.tensor_tensor(out=vg[:], in0=p[:], in1=v0b[:], op=ALU.add), reads=[p, v0b], writes=[vg])
                    c.op("act", lambda e: e.activation(out=vg[:], in_=vg[:], func=AF.Sigmoid), reads=[vg], writes=[vg])
                    m = mk_mix(4)
                    p = lora_tm(m, 96, 96, AF.Copy, a2, lo[2])
                    c.op("dve", lambda e: e.tensor_tensor(out=a_sb[:], in0=p[:], in1=a0b[:], op=ALU.add), reads=[p, a0b], writes=[a_sb])
                    c.op("act", lambda e: e.activation(out=a_sb[:], in_=a_sb[:], func=AF.Sigmoid), reads=[a_sb], writes=[a_sb])
                    m = mk_mix(5)
                    for oc in range(4):
                        half = oc // 2
                        if oc % 2 == 0:
                            w = wb[wi[0] % 3]
                            wi[0] += 1
                            c.dma("sp", w[:], w4_d[3, half], writes=[w])
                        for kc in range(KC):
                            c.op("pe", lambda e: e.matmul(pz[:, oc, :], w[:, kc, (oc % 2) * 128:(oc % 2) * 128 + 128], m[:, kc, :], start=(kc == 0), stop=(kc == KC - 1)),
                                 reads=[w, m], writes=[pz])
                    c.op("act", lambda e: e.activation(out=zs[:], in_=pz[:], func=AF.Silu), reads=[pz], writes=[zs])
                    c.dma("sp", SZT[:, tok0:tok0 + 128].rearrange("(hg p) t -> p hg t", p=128), zs[:], reads=[zs])
                    c.op("pool", lambda e: e.tensor_copy(out=hT[:, :, 0:1], in_=hT[:, :, 128:129]), reads=[hT], writes=[hT])

                    c.op("dve", lambda e: e.tensor_tensor(out=kk[:], in0=k_sb[:], in1=kkb[:], op=ALU.mult), reads=[k_sb, kkb], writes=[kk])
                    c.op("pool", lambda e: e.tensor_tensor(out=tq[:], in0=kk[:], in1=kk[:], op=ALU.mult), reads=[kk], writes=[tq])
                    c.op("dve", lambda e: e.tensor_reduce(out=s8[:], in_=v3(tq), axis=AX.X, op=ALU.add), reads=[tq], writes=[s8])
                    c.op("act", lambda e: e.activation(out=s8[:], in_=s8[:], func=AF.Sqrt), reads=[s8], writes=[s8])
                    c.op("dve", lambda e: e.tensor_scalar(out=s8[:], in0=s8[:], scalar1=1e-12, scalar2=None, op0=ALU.max), reads=[s8], writes=[s8])
                    c.op("dve", lambda e: e.reciprocal(out=r8[:], in_=s8[:]), reads=[s8], writes=[r8])
                    c.op("dve", lambda e: e.tensor_tensor(out=v3(kk), in0=v3(kk), in1=r8[:].unsqueeze(2).to_broadcast([128, 8, 64]), op=ALU.mult),
                         reads=[kk, r8], writes=[kk])
                    c.op("pool", lambda e: e.tensor_scalar(out=av[:], in0=kk[:], scalar1=-1.0, scalar2=None, op0=ALU.mult), reads=[kk], writes=[av])
                    c.op("pool", lambda e: e.tensor_tensor(out=bv_[:], in0=kk[:], in1=a_sb[:], op=ALU.mult), reads=[kk, a_sb], writes=[bv_])
                    c.op("pool", lambda e: e.tensor_tensor(out=tq[:], in0=a_sb[:], in1=kab[:], op=ALU.mult), reads=[a_sb, kab], writes=[tq])
                    c.op("pool", lambda e: e.tensor_tensor(out=tq[:], in0=tq[:], in1=omk[:], op=ALU.add), reads=[tq, omk], writes=[tq])
                    c.op("dve", lambda e: e.tensor_tensor(out=k_sb[:], in0=k_sb[:], in1=tq[:], op=ALU.mult), reads=[k_sb, tq], writes=[k_sb])
                    c.op("dve", lambda e: e.tensor_tensor(out=tq[:], in0=r_sb[:], in1=k_sb[:], op=ALU.mult), reads=[r_sb, k_sb], writes=[tq])
                    c.op("pool", lambda e: e.tensor_tensor(out=tq[:], in0=tq[:], in1=rkb[:], op=ALU.mult), reads=[tq, rkb], writes=[tq])
                    c.op("dve", lambda e: e.tensor_reduce(out=bon[:], in_=v3(tq), axis=AX.X, op=ALU.add), reads=[tq], writes=[bon])
                    c.dma("sp", vf[:], vf_d[tok0:tok0 + 128, :], writes=[vf])
                    c.op("pool", lambda e: e.tensor_tensor(out=vf[:], in0=vf[:], in1=v_sb[:], op=ALU.subtract), reads=[vf, v_sb], writes=[vf])
                    c.op("dve", lambda e: e.tensor_tensor(out=vf[:], in0=vf[:], in1=vg[:], op=ALU.mult), reads=[vf, vg], writes=[vf])
                    c.op("pool", lambda e: e.tensor_tensor(out=v_sb[:], in0=v_sb[:], in1=vf[:], op=ALU.add), reads=[v_sb, vf], writes=[v_sb])
                    c.op("dve", lambda e: e.tensor_tensor(out=v3(vf), in0=v3(v_sb), in1=bon[:].unsqueeze(2).to_broadcast([128, 8, 64]), op=ALU.mult),
                         reads=[v_sb, bon], writes=[vf])
                    for src, dst_sb, dscr_ in ((v_sb, vT_sb, VT), (vf, bT_sb, BVT)):
                        p = ptr[tcnt[0] % 2]
                        tcnt[0] += 1
                        for hg in range(4):
                            c.op("pe", lambda e: e.transpose(out=p[:, hg, :], in_=src[:, hg * 128:(hg + 1) * 128], identity=ident[:]),
                                 reads=[src, ident], writes=[p])
                        c.op("act", lambda e: e.activation(out=dst_sb[:], in_=p[:], func=AF.Copy), reads=[p], writes=[dst_sb])
                        c.dma("sp", dscr_[:, tok0:tok0 + 128].rearrange("(hg p) t -> p hg t", p=128), dst_sb[:], reads=[dst_sb])
                    store_r5(r_sb, 0, tok0)
                    store_r5(w_sb, 1, tok0)
                    store_r5(k_sb, 2, tok0)
                    store_r5(av, 3, tok0)
                    store_r5(bv_, 4, tok0)
                c.barrier()
        if STOP == "b2":
            c.finish()
            return nc

        with contextlib.ExitStack() as ph:
            sel = c.sb("sel", [32, 16, 128], stack=ph)
            rows = [c.sb("rows%d" % i, [32, 1280], stack=ph) for i in range(3)]
            vts = [c.sb("vts%d" % i, [128, 4, 512], stack=ph) for i in range(2)]
            yts = [c.sb("yts%d" % i, [128, 4, 512], stack=ph) for i in range(2)]
            S = c.sb("S", [128, 4, 64], stack=ph)
            tmp = c.sb("tmpS", [128, 4, 64], stack=ph)
            sa = c.sb("sa", [128, 4], stack=ph)
            bcs = [c.ps("bc%d" % i, [128, 3, 512], stack=ph) for i in range(2)]
            c.dma("sp", sel[:], sel_d, writes=[sel])
            c.op("dve", lambda e: e.memset(S[:], 0.0), writes=[S])
            BL = min(512, T)
            for t in range(T):
                if t % BL == 0:
                    vt = vts[(t // BL) % 2]
                    yt = yts[(t // BL) % 2]
                    c.dma("sp", vt[:, :, 0:BL], VT[:, t:t + BL].rearrange("(hg p) t -> p hg t", p=128), writes=[vt])
                if t % 16 == 0:
                    rw = rows[(t // 16) % 3]
                    c.dma("sp", rw[:], R5[t:t + 16].rearrange("t par q hg k -> (t par) (q hg k)"), writes=[rw])
                tb = t % 16
                tl = t % BL
                bc = bcs[t % 2]
                for part, (c0, n) in enumerate(((0, 512), (512, 512), (1024, 256))):
                    c.op("pe", lambda e: e.matmul(bc[:, part, 0:n], sel[:, tb, :], rw[:, c0:c0 + n], start=True, stop=True),
                         reads=[sel, rw], writes=[bc])

                def bq(q):
                    pq_, of_ = [(0, 0), (0, 256), (1, 0), (1, 256), (2, 0)][q]
                    return bc[:, pq_, of_:of_ + 256].rearrange("p (a b) -> p a b", a=4)
                c.op("dve", lambda e: e.tensor_tensor(out=tmp[:], in0=S[:], in1=bq(3), op=ALU.mult), reads=[S, bc], writes=[tmp])
                c.op("dve", lambda e: e.tensor_reduce(out=sa[:], in_=tmp[:], axis=AX.X, op=ALU.add), reads=[tmp], writes=[sa])
                c.op("dve", lambda e: e.tensor_tensor(out=S[:], in0=S[:], in1=bq(1), op=ALU.mult), reads=[S, bc], writes=[S])
                c.op("dve", lambda e: e.tensor_tensor(out=tmp[:], in0=bq(4), in1=sa[:].unsqueeze(2).to_broadcast([128, 4, 64]), op=ALU.mult),
                     reads=[bc, sa], writes=[tmp])
                c.op("dve", lambda e: e.tensor_tensor(out=S[:], in0=S[:], in1=tmp[:], op=ALU.add), reads=[S, tmp], writes=[S])
                c.op("dve", lambda e: e.tensor_tensor(out=tmp[:], in0=bq(2), in1=vt[:, :, tl:tl + 1].to_broadcast([128, 4, 64]), op=ALU.mult),
                     reads=[bc, vt], writes=[tmp])
                c.op("dve", lambda e: e.tensor_tensor(out=S[:], in0=S[:], in1=tmp[:], op=ALU.add), reads=[S, tmp], writes=[S])
                c.op("dve", lambda e: e.tensor_tensor(out=tmp[:], in0=S[:], in1=bq(0), op=ALU.mult), reads=[S, bc], writes=[tmp])
                c.op("dve", lambda e: e.tensor_reduce(out=yt[:, :, tl:tl + 1], in_=tmp[:], axis=AX.X, op=ALU.add), reads=[tmp], writes=[yt])
                if tl == BL - 1:
                    c.dma("sp", YT[:, t + 1 - BL:t + 1].rearrange("(hg p) t -> p hg t", p=128), yt[:, :, 0:BL], reads=[yt])
            c.barrier()
        if STOP == "b3":
            c.finish()
            return nc

        with contextlib.ExitStack() as ph:
            lnx = c.sb("lnx", [128, 2, 4], stack=ph)
            c.dma("sp", lnx[:], lnx_d, writes=[lnx])
            ys = [c.sb("y%d" % i, [128, 512], stack=ph) for i in range(2)]
            bvs = [c.sb("bvl%d" % i, [128, 512], stack=ph) for i in range(2)]
            szs = [c.sb("szl%d" % i, [128, 512], stack=ph) for i in range(2)]
            dd = c.sb("dd", [128, 512], stack=ph)
            d2 = c.sb("d2", [128, 512], stack=ph)
            rs = c.sb("rs", [128, 512], stack=ph)
            ygs = [c.sb("yg%d" % i, [128, 512], stack=ph) for i in range(2)]
            pm = c.ps("pmean", [128, 512], stack=ph)
            pvr = c.ps("pvar", [128, 512], stack=ph)
            BL = min(512, T)
            i = 0
            for t0 in range(0, T, BL):
                for hg in range(4):
                    y = ys[i % 2]; bvl = bvs[i % 2]; szl = szs[i % 2]; yg = ygs[i % 2]
                    i += 1
                    rsl = slice(hg * 128, (hg + 1) * 128)
                    c.dma("sp", y[:, 0:BL], YT[rsl, t0:t0 + BL], writes=[y])
                    c.dma("sp", bvl[:, 0:BL], BVT[rsl, t0:t0 + BL], writes=[bvl])
                    c.dma("sp", szl[:, 0:BL], SZT[rsl, t0:t0 + BL], writes=[szl])
                    c.op("pe", lambda e: e.matmul(pm[:, 0:BL], bones[:], y[:, 0:BL], start=True, stop=True), reads=[bones, y], writes=[pm])
                    c.op("dve", lambda e: e.scalar_tensor_tensor(out=dd[:, 0:BL], in0=pm[:, 0:BL], scalar=-1.0 / 64.0, in1=y[:, 0:BL], op0=ALU.mult, op1=ALU.add),
                         reads=[pm, y], writes=[dd])
                    c.op("act", lambda e: e.activation(out=d2[:, 0:BL], in_=dd[:, 0:BL], func=AF.Square), reads=[dd], writes=[d2])
                    c.op("pe", lambda e: e.matmul(pvr[:, 0:BL], bones[:], d2[:, 0:BL], start=True, stop=True), reads=[bones, d2], writes=[pvr])
                    c.op("act", lambda e: e.activation(out=rs[:, 0:BL], in_=pvr[:, 0:BL], func=AF.Sqrt, scale=1.0 / 64.0, bias=cst[:, 1:2]), reads=[pvr, cst], writes=[rs])
                    c.op("dve", lambda e: e.reciprocal(out=rs[:, 0:BL], in_=rs[:, 0:BL]), reads=[rs], writes=[rs])
                    c.op("dve", lambda e: e.tensor_tensor(out=dd[:, 0:BL], in0=dd[:, 0:BL], in1=rs[:, 0:BL], op=ALU.mult), reads=[dd, rs], writes=[dd])
                    c.op("act", lambda e: e.activation(out=dd[:, 0:BL], in_=dd[:, 0:BL], func=AF.Identity, scale=lnx[:, 0, hg:hg + 1], bias=lnx[:, 1, hg:hg + 1]),
                         reads=[dd, lnx], writes=[dd])
                    c.op("pool", lambda e: e.tensor_tensor(out=dd[:, 0:BL], in0=dd[:, 0:BL], in1=bvl[:, 0:BL], op=ALU.add), reads=[dd, bvl], writes=[dd])
                    c.op("dve", lambda e: e.tensor_tensor(out=yg[:, 0:BL], in0=dd[:, 0:BL], in1=szl[:, 0:BL], op=ALU.mult), reads=[dd, szl], writes=[yg])
                    c.dma("sp", ygt_d.sl(rsl, t0, BL), yg[:, 0:BL], reads=[yg])
            c.barrier()
        if not fused:
            c.finish()
    return nc


def build_C(TQ):
    nc = bass.Bass("TRN2", target_bir_lowering=False)
    NT = TQ // 128

    def din(name, shape):
        return nc.dram_tensor(name, list(shape), F32, kind="ExternalInput").ap()

    x_d = din("x", [TQ, D])
    ogt_d = TokWhole(din("ogt", [D, TQ]))
    ygt_d = TokWhole(din("ygt", [D, TQ]))
    ca_d = din("ca", [128, KC])
    wmod0_d = din("wmod0g", [4, 128, KC, 512])
    bmod0_d = din("bmod0", [1, 6144])
    wmod1_d = din("wmod1g", [4, 128, KC, 512])
    bmod1_d = din("bmod1", [1, 6144])
    wout0_d = din("wout0", [128, KC, D])
    wout1_d = din("wout1", [128, KC, D])
    out_d = nc.dram_tensor("out", [TQ, D], F32, kind="ExternalOutput").ap()
    X1 = nc.dram_tensor("x1_s", [TQ, D], F32, kind="Internal").ap()
    with contextlib.ExitStack() as st:
        c = Ctx(nc, st)
        for (wm, bm, src, a_d, wo, dst) in ((wmod0_d, bmod0_d, x_d, ogt_d, wout0_d, X1), (wmod1_d, bmod1_d, X1, ygt_d, wout1_d, out_d)):
            with contextlib.ExitStack() as g0:
                gate_bc = c.sb("gate", [128, D], stack=g0)
                phase_mod(c, ca_d, lambda pc: wm[pc - 8], bm, lambda pc: (gate_bc, (pc - 8) * 512), [8, 9, 10, 11])
                phase_resid(c, NT, lambda i: src[i * 128:(i + 1) * 128, :], lambda i: fm_rows(a_d, i * 128), wo, gate_bc,
                            lambda i, o: c.dma("sp", dst[i * 128:(i + 1) * 128, :], o[:], reads=[o]))
        c.finish()
    return nc


def sel_const():
    sel = np.zeros((32, 16, 128), np.float32)
    for tb in range(16):
        for par in range(2):
            sel[tb * 2 + par, tb, par * 64:(par + 1) * 64] = 1.0
    return sel


def kcp(w):
    return np.ascontiguousarray(w.reshape(KC, 128, w.shape[1]).transpose(1, 0, 2))


def inputs_B(inp, T, ogt_full, v_cores):
    ident, bones, _ = consts()
    sel = sel_const()
    wmod0g = np.ascontiguousarray(wpieces(inp["w_mod"][0][:, 2 * D:], 512))
    wmod1 = np.ascontiguousarray(wpieces(inp["w_mod"][1][:, :2 * D], 512))
    wout = kcp(inp["fox_w_out"][0])
    mu = np.ascontiguousarray(inp["rwkv_mu"][0].reshape(6, KC, 128).transpose(2, 0, 1))
    l1 = kcp(np.concatenate([inp["rwkv_w1"][0], inp["rwkv_a1"][0], inp["rwkv_v1"][0]], axis=1))
    maps = []
    for core in range(8):
        b, g = core // 4, core % 4
        cs = slice(g * 512, (g + 1) * 512)
        w4 = np.stack([np.ascontiguousarray(inp["rwkv_w_in"][0][i][:, cs].reshape(KC, 128, 2, 256).transpose(2, 1, 0, 3)) for i in range(4)], 0)
        pvec = np.stack([inp["rwkv_w0"][0][cs], inp["rwkv_a0"][0][cs], inp["rwkv_v0"][0][cs], inp["rwkv_k_k"][0][cs],
                         inp["rwkv_k_a"][0][cs], inp["rwkv_r_k"][0].reshape(-1)[cs]], 0)
        lnx = np.stack([inp["rwkv_lnx_w"][0][cs].reshape(4, 128).T, inp["rwkv_lnx_b"][0][cs].reshape(4, 128).T], 1)
        maps.append({
            "x": np.ascontiguousarray(inp["x"][b, :T]),
            "ogt": ogt_full[b],
            "ca": fm(inp["c"][b]),
            "wmod0g": wmod0g, "bmod0": np.ascontiguousarray(inp["b_mod"][0][None, :]),
            "wmod1": wmod1, "bmod1": np.ascontiguousarray(inp["b_mod"][1][None, :]),
            "normw": np.ascontiguousarray(inp["norm_w"][1][None, :]),
            "wout": wout, "mu": mu, "w4": np.ascontiguousarray(w4), "l1": l1,
            "w2": np.ascontiguousarray(inp["rwkv_w2"][0][:, cs]), "a2": np.ascontiguousarray(inp["rwkv_a2"][0][:, cs]),
            "v2": np.ascontiguousarray(inp["rwkv_v2"][0][:, cs]),
            "pvec": np.ascontiguousarray(pvec.astype(np.float32)), "lnx": np.ascontiguousarray(lnx.astype(np.float32)),
            "vfirst": v_cores[core], "ident": ident, "bones": bones, "sel": sel,
        })
    return maps


def inputs_C(inp, T, ogt_full, ygt_full):
    TQ = T // 4
    wmod0g = np.ascontiguousarray(wpieces(inp["w_mod"][0][:, 2 * D:], 512))
    wmod1g = np.ascontiguousarray(wpieces(inp["w_mod"][1][:, 2 * D:], 512))
    wout0 = kcp(inp["fox_w_out"][0])
    wout1 = kcp(inp["rwkv_w_out"][0])
    maps = []
    for core in range(8):
        b, g = core // 4, core % 4
        ts = slice(g * TQ, (g + 1) * TQ)
        maps.append({
            "x": np.ascontiguousarray(inp["x"][b, ts]),
            "ogt": np.ascontiguousarray(ogt_full[b][:, ts]),
            "ygt": np.ascontiguousarray(ygt_full[b][:, ts]),
            "ca": fm(inp["c"][b]),
            "wmod0g": wmod0g, "bmod0": np.ascontiguousarray(inp["b_mod"][0][None, :]),
            "wmod1g": wmod1g, "bmod1": np.ascontiguousarray(inp["b_mod"][1][None, :]),
            "wout0": wout0, "wout1": wout1,
        })
    return maps


def run_all(inp, T):
    inp = {k: np.asarray(v, dtype=np.float32) for k, v in inp.items()}
    resA = run_bass_kernel_spmd(build_A(T), inputs_A(inp, T), core_ids=list(range(8))).results
    ogt_full = [np.ascontiguousarray(np.concatenate([resA[b * 4 + g]["ogt"] for g in range(4)], 0)) for b in range(2)]
    v_cores = [np.ascontiguousarray(resA[core]["v"]) for core in range(8)]
    resB = run_bass_kernel_spmd(build_B(T), inputs_B(inp, T, ogt_full, v_cores), core_ids=list(range(8))).results
    ygt_full = [np.ascontiguousarray(np.concatenate([resB[b * 4 + g]["ygt"] for g in range(4)], 0)) for b in range(2)]
    resC = run_bass_kernel_spmd(build_C(T // 4), inputs_C(inp, T, ogt_full, ygt_full), core_ids=list(range(8))).results
    TQ = T // 4
    out = np.zeros((2, T, D), np.float32)
    for core in range(8):
        b, g = core // 4, core % 4
        out[b, g * TQ:(g + 1) * TQ] = resC[core]["out"]
    return out


def build_F(T):
    nc = bass.Bass("TRN2", target_bir_lowering=False)
    NT = T // 128

    def ext(name, shape):
        return nc.dram_tensor(name, list(shape), F32, kind="ExternalInput").ap()

    def scr(name, shape):
        return nc.dram_tensor(name, list(shape), F32, kind="Internal").ap()

    share = {}
    share["x"] = ext("x", [T, D])
    share["ca"] = ext("ca", [128, KC])
    share["ident"] = ext("ident", [128, 128])
    share["bones"] = ext("bones", [128, 128])
    wmod0 = ext("wmod0", [12, 128, KC, 512])
    wmod1 = ext("wmod1", [12, 128, KC, 512])
    share["wmod"] = wmod0
    share["wmod0g"] = wmod0[8:12]
    share["wmod1"] = wmod1
    share["bmod"] = ext("bmod0", [1, 6144])
    share["bmod0"] = share["bmod"]
    share["bmod1"] = ext("bmod1", [1, 6144])
    wout1_d = ext("wout1", [128, KC, D])
    NP = T // 512
    og_own = [scr("ogo%d" % i, [512, 512]) for i in range(NP)]
    og_full = [scr("ogf%d" % i, [D, 512]) for i in range(NP)]
    yg_own = [scr("ygo%d" % i, [512, 512]) for i in range(NP)]
    yg_full = [scr("ygf%d" % i, [D, 512]) for i in range(NP)]
    share["A_ogt"] = TokPieces(og_own)
    share["A_v"] = scr("v_own", [T, 512])
    share["ogt"] = TokPieces(og_full)
    share["vfirst"] = share["A_v"]
    share["B_ygt"] = TokPieces(yg_own)
    share["x1_s"] = scr("x1_s", [T, D])
    ygt_full = TokPieces(yg_full)
    out_d = nc.dram_tensor("out", [T, D], F32, kind="ExternalOutput").ap()
    groups = [[0, 1, 2, 3], [4, 5, 6, 7]]
    with contextlib.ExitStack() as st:
        c = Ctx(nc, st)
        fused = dict(nc=nc, c=c, share=share)
        ccsem = st.enter_context(nc.semaphore("ccsem"))

        ccn = [0]

        def allgather(srcs, dsts):
            for src, dst in zip(srcs, dsts):
                nc.gpsimd.collective_compute("AllGather", ALU.bypass, replica_groups=groups, ins=[src], outs=[dst]).then_inc(ccsem, 1)
                ccn[0] += 1
            nc.gpsimd.wait_ge(ccsem, ccn[0])
            c.barrier()

        build_A(T, fused)
        allgather(og_own, og_full)
        build_B(T, fused)
        allgather(yg_own, yg_full)
        X1 = share["x1_s"]
        with contextlib.ExitStack() as g0:
            gate_bc = c.sb("gate1", [128, D], stack=g0)
            phase_mod(c, share["ca"], lambda pc: wmod1[pc], share["bmod1"], lambda pc: (gate_bc, (pc - 8) * 512), [8, 9, 10, 11])
            phase_resid(c, NT, lambda i: X1[i * 128:(i + 1) * 128, :], lambda i: fm_rows(ygt_full, i * 128), wout1_d, gate_bc,
                        lambda i, o: c.dma("sp", out_d[i * 128:(i + 1) * 128, :], o[:], reads=[o]))
        c.finish()
    return nc


def inputs_F(inp, T):
    mA = inputs_A(inp, T)
    dummy = [np.zeros((D, T), np.float32)] * 2
    mB = inputs_B(inp, T, dummy, [np.zeros((T, 512), np.float32)] * 8)
    wmod0 = np.ascontiguousarray(wpieces(inp["w_mod"][0], 512))
    wmod1 = np.ascontiguousarray(wpieces(inp["w_mod"][1], 512))
    wout1 = kcp(inp["rwkv_w_out"][0])
    maps = []
    for core in range(8):
        a, b = mA[core], mB[core]
        m = {"x": a["x"], "ca": a["ca"], "ident": a["ident"], "bones": a["bones"], "wmod0": wmod0, "wmod1": wmod1,
             "bmod0": a["bmod"], "bmod1": b["bmod1"], "wout1": wout1}
        for k in ("normw", "wqkz", "wv", "wf", "bf", "qkw", "masks"):
            m["A_" + k] = a[k]
        for k in ("normw", "wout", "mu", "w4", "l1", "w2", "a2", "v2", "pvec", "lnx", "sel"):
            m["B_" + k] = b[k]
        maps.append(m)
    return maps


def run_fused(inp, T):
    inp = {k: np.asarray(v, dtype=np.float32) for k, v in inp.items()}
    res = run_bass_kernel_spmd(build_F(T), inputs_F(inp, T), core_ids=list(range(8))).results
    return np.stack([res[0]["out"], res[4]["out"]], 0)


def kernel(**inputs):
    return run_fused(inputs, 16384)
```

```python
import contextlib
import numpy as np
import concourse.bass as bass
import concourse.mybir as mybir
from concourse.bass_utils import run_bass_kernel_spmd

F32 = mybir.dt.float32
BF16 = mybir.dt.bfloat16
AF = mybir.ActivationFunctionType
ALU = mybir.AluOpType
AX = mybir.AxisListType

D = 2048
KC = 16
NORM_EPS = 1e-6
LNX_EPS = 64e-5
NEG = -30000.0
STOP = ""
SKIP = set()


class Buf:
    def __init__(self, name, t=None):
        self.name = name
        self.t = t
        self.w = None
        self.r = {}

    def __getitem__(self, k):
        return self.t[k]


class Ctx:
    ENGS = ("pe", "act", "dve", "pool", "sp")

    def __init__(self, nc, stack):
        self.nc = nc
        self.stack = stack
        self.e = {"pe": nc.tensor, "act": nc.scalar, "dve": nc.vector,
                  "pool": nc.gpsimd, "sp": nc.sync}
        self.sems = {}
        self.cnt = {}
        self.seen = {k: {} for k in self.ENGS}
        self.nsem = 0
        self.uid = 0
        self.epoch = {}
        self.free = []
        for k in self.ENGS:
            self._newsem(k)

    def _newsem(self, key):
        h = self.stack.enter_context(self.nc.semaphore("s%d" % self.nsem))
        self.nsem += 1
        self.sems[key] = h
        self.cnt[key] = 0
        return h

    def sb(self, name, shape, dtype=F32, stack=None):
        self.uid += 1
        nm = "%s_%d" % (name, self.uid)
        t = (stack or self.stack).enter_context(self.nc.sbuf_tensor(nm, list(shape), dtype))
        return Buf(nm, t)

    def ps(self, name, shape, dtype=F32, stack=None):
        self.uid += 1
        nm = "%s_%d" % (name, self.uid)
        t = (stack or self.stack).enter_context(self.nc.psum_tensor(nm, list(shape), dtype))
        return Buf(nm, t)

    def _wait(self, eng, ev):
        if ev is None:
            return
        key, val = ev
        if self.seen[eng].get(key, 0) >= val:
            return
        self.e[eng].wait_ge(self.sems[key], val)
        self.seen[eng][key] = val

    def _deps(self, eng, reads, writes):
        for b in reads:
            if b.w is not None and not (eng == "pe" and b.w[0] == "pe"):
                self._wait(eng, b.w)
        for b in writes:
            if b.w is not None and not (eng == "pe" and b.w[0] == "pe"):
                self._wait(eng, b.w)
            for key, val in b.r.items():
                if key == eng and eng == "pe":
                    continue
                self._wait(eng, (key, val))

    def _mark(self, ev, reads, writes):
        key, val = ev
        for b in reads:
            if b.r.get(key, 0) < val:
                b.r[key] = val
        for b in writes:
            b.w = ev
            b.r = {}

    def op(self, eng, fn, reads=(), writes=()):
        self._deps(eng, reads, writes)
        inst = fn(self.e[eng])
        self.cnt[eng] += 1
        inst.then_inc(self.sems[eng], 1)
        ev = (eng, self.cnt[eng])
        self._mark(ev, reads, writes)
        return ev

    def dma(self, q, out_ap, in_ap, reads=(), writes=(), **kw):
        self._deps(q, reads, writes)
        owner = (list(writes) + list(reads))[0]
        bkey = ("dma", owner.name, q)
        key = self.epoch.get(bkey)
        if key is None:
            if self.free:
                key = self.free.pop()
            else:
                key = ("dsem", self.nsem)
                self._newsem(key)
            self.epoch[bkey] = key
        inst = self.e[q].dma_start(out=out_ap, in_=in_ap, **kw)
        self.cnt[key] += 16
        inst.then_inc(self.sems[key], 16)
        ev = (key, self.cnt[key])
        self._mark(ev, reads, writes)
        return ev

    def barrier(self):
        for key, val in self.cnt.items():
            if val > 0:
                self._wait("sp", (key, val))
        self.nc.all_engine_barrier()
        for k in self.ENGS:
            for key, val in self.cnt.items():
                self.seen[k][key] = val
        self.free.extend(self.epoch.values())
        self.epoch.clear()

    def finish(self):
        for key, val in self.cnt.items():
            if val > 0:
                self._wait("sp", (key, val))


class TokWhole:
    def __init__(self, ap):
        self.ap = ap

    def sl(self, rows, t0, n):
        return self.ap[rows, t0:t0 + n]


class TokPieces:
    def __init__(self, aps):
        self.aps = aps

    def sl(self, rows, t0, n):
        p, o = divmod(t0, 512)
        assert o + n <= 512
        return self.aps[p][rows, o:o + n]


def bcast_rows(ap_row):
    return ap_row.partition_broadcast(128)


def phase_mod(c, ca_d, wmod_d, bmod_d, dst, pieces):
    with contextlib.ExitStack() as ph:
        ca = c.sb("ca", [128, KC], stack=ph)
        sg = c.sb("sg", [128, KC], stack=ph)
        ca_rep = c.sb("carep", [128, KC, 128], stack=ph)
        bm = c.sb("bm", [128, 6144], stack=ph)
        wp = [c.sb("wp%d" % i, [128, KC, 512], stack=ph) for i in range(2)]
        pm = [c.ps("pm%d" % i, [128, 512], stack=ph) for i in range(2)]
        c.dma("sp", ca[:], ca_d, writes=[ca])
        c.dma("sp", bm[:].unsqueeze(1), bcast_rows(bmod_d), writes=[bm])
        c.op("act", lambda e: e.activation(out=sg[:], in_=ca[:], func=AF.Sigmoid), reads=[ca], writes=[sg])
        c.op("dve", lambda e: e.tensor_tensor(out=sg[:], in0=sg[:], in1=ca[:], op=ALU.mult), reads=[sg, ca], writes=[sg])
        c.op("dve", lambda e: e.tensor_copy(out=ca_rep[:], in_=sg[:].unsqueeze(2).to_broadcast([128, KC, 128])),
             reads=[sg], writes=[ca_rep])
        for i, pc in enumerate(pieces):
            w = wp[i % 2]
            p = pm[i % 2]
            c.dma("sp", w[:], wmod_d(pc) if callable(wmod_d) else wmod_d[pc], writes=[w])
            for kc in range(KC):
                c.op("pe", lambda e: e.matmul(p[:], ca_rep[:, kc, :], w[:, kc, :], start=(kc == 0), stop=(kc == KC - 1)),
                     reads=[ca_rep, w], writes=[p])
            db, c0 = dst(pc)
            c.op("dve", lambda e: e.tensor_tensor(out=db[:, c0:c0 + 512], in0=p[:],
                                                  in1=bm[:, pc * 512:(pc + 1) * 512], op=ALU.add),
                 reads=[p, bm], writes=[db])
        c.barrier()


def make_A(c, normw_d, A_bc):
    with contextlib.ExitStack() as ph:
        nw = c.sb("nw", [128, D], stack=ph)
        c.dma("sp", nw[:].unsqueeze(1), bcast_rows(normw_d), writes=[nw])
        c.op("dve", lambda e: e.scalar_tensor_tensor(out=A_bc[:], in0=A_bc[:], scalar=1.0, in1=nw[:],
                                                     op0=ALU.add, op1=ALU.mult), reads=[A_bc, nw], writes=[A_bc])
        c.barrier()


def norm_tile(c, x_ap, xt, ss, sd, rstd, hm, A_bc, shift_bc):
    c.dma("sp", xt[:], x_ap, writes=[xt])
    c.op("act", lambda e: e.activation(out=hm[:], in_=xt[:], func=AF.Square, accum_out=ss[:]), reads=[xt], writes=[hm, ss])
    c.op("act", lambda e: e.activation(out=sd[:], in_=ss[:], func=AF.Sqrt, scale=1.0 / D, bias=NORM_EPS), reads=[ss], writes=[sd])
    c.op("dve", lambda e: e.reciprocal(out=rstd[:], in_=sd[:]), reads=[sd], writes=[rstd])
    c.op("dve", lambda e: e.scalar_tensor_tensor(out=hm[:], in0=xt[:], scalar=rstd[:, 0:1], in1=A_bc[:],
                                                 op0=ALU.mult, op1=ALU.mult), reads=[xt, rstd, A_bc], writes=[hm])
    c.op("pool", lambda e: e.tensor_tensor(out=hm[:], in0=hm[:], in1=shift_bc[:], op=ALU.add), reads=[hm, shift_bc], writes=[hm])


def transpose_tile(c, hm, hT, col0, ident, ptr, cnt):
    for g4 in range(4):
        p = ptr[cnt[0] % 2]
        cnt[0] += 1
        for j in range(4):
            kc = g4 * 4 + j
            c.op("pe", lambda e: e.transpose(out=p[:, j, :], in_=hm[:, kc * 128:(kc + 1) * 128], identity=ident[:]),
                 reads=[hm, ident], writes=[p])
        if g4 % 2 == 0:
            c.op("act", lambda e: e.activation(out=hT[:, g4 * 4:(g4 + 1) * 4, col0:col0 + 128], in_=p[:], func=AF.Copy),
                 reads=[p], writes=[hT])
        else:
            c.op("dve", lambda e: e.tensor_copy(out=hT[:, g4 * 4:(g4 + 1) * 4, col0:col0 + 128], in_=p[:]),
                 reads=[p], writes=[hT])


def build_A(T, fused=None):
    nc = fused["nc"] if fused else bass.Bass("TRN2", target_bir_lowering=False)
    NB = T // 128
    CH = 256
    NCH = T // CH
    QC = 512
    NQ = T // QC

    def din(name, shape):
        if fused:
            if name in fused["share"]:
                return fused["share"][name]
            return nc.dram_tensor("A_" + name, list(shape), F32, kind="ExternalInput").ap()
        return nc.dram_tensor(name, list(shape), F32, kind="ExternalInput").ap()

    x_d = din("x", [T, D])
    ca_d = din("ca", [128, KC])
    wmod_d = din("wmod", [12, 128, KC, 512])
    bmod_d = din("bmod", [1, 6144])
    normw_d = din("normw", [1, D])
    wqkz_d = din("wqkz", [12, 128, KC, 128])
    wv_d = din("wv", [128, KC, 512])
    wf_d = din("wf", [128, KC, 8])
    bf_d = din("bf", [8, 1])
    qkw_d = din("qkw", [128, 2])
    ident_d = din("ident", [128, 128])
    bones_d = din("bones", [128, 128])
    masks_d = din("masks", [128, 4, 512])
    if fused:
        ogt_d = fused["share"]["A_ogt"]
        v_d = fused["share"]["A_v"]
    else:
        ogt_d = TokWhole(nc.dram_tensor("ogt", [512, T], F32, kind="ExternalOutput").ap())
        v_d = nc.dram_tensor("v", [T, 512], F32, kind="ExternalOutput").ap()
    qt_d = nc.dram_tensor("qt_s", [512, T], F32, kind="Internal").ap()
    kt_d = nc.dram_tensor("kt_s", [512, T], F32, kind="Internal").ap()
    zt_d = nc.dram_tensor("zt_s", [512, T], F32, kind="Internal").ap()
    cum_d = nc.dram_tensor("cum_s", [8, T], F32, kind="Internal").ap()

    with contextlib.ExitStack() as st:
        c = fused["c"] if fused else Ctx(nc, st)
        ident = c.sb("ident", [128, 128], stack=st)
        bones = c.sb("bones", [128, 128], stack=st)
        ncT = c.sb("ncT", [128, NB, 8], stack=st)
        c.dma("sp", ident[:], ident_d, writes=[ident])
        c.dma("sp", bones[:], bones_d, writes=[bones])
        cst = c.sb("cst", [128, 4], stack=st)
        c.op("dve", lambda e: e.memset(cst[:, 0:1], NORM_EPS), writes=[cst])
        c.op("dve", lambda e: e.memset(cst[:, 1:2], 64.0 * NORM_EPS), writes=[cst])
        c.op("dve", lambda e: e.memset(cst[:, 2:3], 1.0), writes=[cst])
        with contextlib.ExitStack() as g1:
            shift_bc = c.sb("shiftbc", [128, D], stack=g1)
            A_bc = c.sb("Abc", [128, D], stack=g1)
            phase_mod(c, ca_d, wmod_d, bmod_d,
                      lambda pc: (shift_bc, pc * 512) if pc < 4 else (A_bc, (pc - 4) * 512), list(range(8)))
            make_A(c, normw_d, A_bc)
            if STOP == "p0":
                c.op("dve", lambda e: e.tensor_copy(out=ident[:], in_=shift_bc[:, 0:128]), reads=[shift_bc], writes=[ident])
                c.dma("sp", v_d[0:128, 0:128], ident[:], reads=[ident])
                c.op("dve", lambda e: e.tensor_copy(out=bones[:], in_=A_bc[:, 0:128]), reads=[A_bc], writes=[bones])
                c.dma("sp", v_d[0:128, 128:256], bones[:], reads=[bones])
                c.finish()
                return nc

            for sweep in range(2):
                with contextlib.ExitStack() as ph:
                    noc = 8 if sweep == 0 else 4
                    wq = c.sb("wq", [128, noc, KC, 128], stack=ph)
                    hT = c.sb("hT", [128, KC, CH], stack=ph)
                    xts = [c.sb("xt%d" % i, [128, D], stack=ph) for i in range(2)]
                    hm = c.sb("hm", [128, D], stack=ph)
                    sss = [c.sb("ss%d" % i, [128, 1], stack=ph) for i in range(2)]
                    sd = c.sb("sd", [128, 1], stack=ph)
                    rstd = c.sb("rstd", [128, 1], stack=ph)
                    qn = [c.sb("qn%d" % i, [128, CH], stack=ph) for i in range(3)]
                    ptr = [c.ps("ptr%d" % i, [128, 4, 128], stack=ph) for i in range(2)]
                    pq = [c.ps("pq%d" % i, [128, CH], stack=ph) for i in range(2)]
                    for oc in range(noc):
                        c.dma("sp", wq[:, oc, :, :], wqkz_d[oc + (0 if sweep == 0 else 8)], writes=[wq])
                    if sweep == 0:
                        wf = c.sb("wf", [128, KC, 8], stack=ph)
                        bf = c.sb("bf", [8, 1], stack=ph)
                        nbf = c.sb("nbf", [8, 1], stack=ph)
                        qkw = c.sb("qkw", [128, 2], stack=ph)
                        ones8 = c.sb("ones8", [8, CH], stack=ph)
                        sq = [c.sb("sq%d" % i, [128, CH], stack=ph) for i in range(2)]
                        qraw = [c.sb("qraw%d" % i, [128, CH], stack=ph) for i in range(2)]
                        sd2 = [c.sb("sd2%d" % i, [128, CH], stack=ph) for i in range(2)]
                        fe = c.sb("fe", [8, CH], stack=ph)
                        cums = [c.sb("cum%d" % i, [8, CH], stack=ph) for i in range(2)]
                        pss = [c.ps("pss%d" % i, [128, CH], stack=ph) for i in range(2)]
                        pf = c.ps("pf", [128, 512], stack=ph)
                        c.dma("sp", wf[:], wf_d, writes=[wf])
                        c.dma("sp", bf[:], bf_d, writes=[bf])
                        c.dma("sp", qkw[:], qkw_d, writes=[qkw])
                        c.op("dve", lambda e: e.tensor_scalar(out=nbf[:], in0=bf[:], scalar1=-1.0, scalar2=None, op0=ALU.mult), reads=[bf], writes=[nbf])
                        c.op("dve", lambda e: e.memset(ones8[:], 1.0), writes=[ones8])
                    else:
                        wv = c.sb("wv", [128, KC, 512], stack=ph)
                        vt = [c.sb("vt%d" % i, [128, 512], stack=ph) for i in range(2)]
                        pv = [c.ps("pv%d" % i, [128, 512], stack=ph) for i in range(2)]
                        c.dma("sp", wv[:], wv_d, writes=[wv])
                    tcnt = [0]
                    it = 0
                    for ch in range(NCH):
                        for tt in range(2):
                            tok0 = ch * CH + tt * 128
                            norm_tile(c, x_d[tok0:tok0 + 128, :], xts[it % 2], sss[it % 2], sd, rstd, hm, A_bc, shift_bc)
                            transpose_tile(c, hm, hT, tt * 128, ident, ptr, tcnt)
                            it += 1
                        tsl = slice(ch * CH, (ch + 1) * CH)
                        for oc in range(noc):
                            p = pq[oc % 2]
                            for kc in range(KC):
                                c.op("pe", lambda e: e.matmul(p[:], wq[:, oc, kc, :], hT[:, kc, :], start=(kc == 0), stop=(kc == KC - 1)),
                                     reads=[wq, hT], writes=[p])
                            o = qn[oc % 3]
                            if sweep == 0 and "qk" not in SKIP:
                                s_ = sq[oc % 2]; qr = qraw[oc % 2]; s2 = sd2[oc % 2]; pz = pss[oc % 2]
                                c.op("act", lambda e: e.activation(out=s_[:], in_=p[:], func=AF.Square), reads=[p], writes=[s_])
                                c.op("act", lambda e: e.activation(out=qr[:], in_=p[:], func=AF.Copy), reads=[p], writes=[qr])
                                c.op("pe", lambda e: e.matmul(pz[:], bones[:], s_[:], start=True, stop=True), reads=[bones, s_], writes=[pz])
                                if oc < 4:
                                    c.op("act", lambda e: e.activation(out=s2[:], in_=pz[:], func=AF.Sqrt, scale=1.0, bias=cst[:, 1:2]), reads=[pz, cst], writes=[s2])
                                else:
                                    c.op("act", lambda e: e.activation(out=s2[:], in_=pz[:], func=AF.Sqrt, scale=1.0 / 64.0, bias=cst[:, 0:1]), reads=[pz, cst], writes=[s2])
                                c.op("dve", lambda e: e.reciprocal(out=s2[:], in_=s2[:]), reads=[s2], writes=[s2])
                                wcol = 0 if oc < 4 else 1
                                c.op("dve", lambda e: e.scalar_tensor_tensor(out=o[:], in0=qr[:], scalar=qkw[:, wcol:wcol + 1], in1=s2[:],
                                                                             op0=ALU.mult, op1=ALU.mult), reads=[qr, qkw, s2], writes=[o])
                                dst = qt_d if oc < 4 else kt_d
                            else:
                                c.op("act", lambda e: e.activation(out=o[:], in_=p[:], func=AF.Copy), reads=[p], writes=[o])
                                dst = zt_d if sweep == 1 else (qt_d if oc < 4 else kt_d)
                            r0 = (oc % 4) * 128
                            c.dma("sp", dst[r0:r0 + 128, tsl], o[:], reads=[o])
                        if sweep == 1:
                            for tt in range(2):
                                p = pv[tt % 2]
                                for kc in range(KC):
                                    c.op("pe", lambda e: e.matmul(p[:], hT[:, kc, tt * 128:(tt + 1) * 128], wv[:, kc, :], start=(kc == 0), stop=(kc == KC - 1)),
                                         reads=[hT, wv], writes=[p])
                                o = vt[tt % 2]
                                c.op("act", lambda e: e.activation(out=o[:], in_=p[:], func=AF.Copy), reads=[p], writes=[o])
                                tok0 = ch * CH + tt * 128
                                c.dma("sp", v_d[tok0:tok0 + 128, :], o[:], reads=[o])
                        elif "f" not in SKIP:
                            for kc in range(KC):
                                c.op("pe", lambda e: e.matmul(pf[0:8, 0:CH], wf[:, kc, :], hT[:, kc, :], start=(kc == 0), stop=(kc == KC - 1)),
                                     reads=[wf, hT], writes=[pf])
                            c.op("act", lambda e: e.activation(out=fe[:], in_=pf[0:8, 0:CH], func=AF.Exp, scale=-1.0, bias=nbf[:, 0:1]), reads=[pf, nbf], writes=[fe])
                            c.op("act", lambda e: e.activation(out=fe[:], in_=fe[:], func=AF.Ln, scale=1.0, bias=cst[0:8, 2:3]), reads=[fe, cst], writes=[fe])
                            cu = cums[ch % 2]
                            prev = cums[(ch + 1) % 2]
                            init = 0.0 if ch == 0 else prev[:, CH - 1:CH]
                            c.op("dve", lambda e: e.tensor_tensor_scan(out=cu[:], data0=ones8[:], data1=fe[:], initial=init,
                                                                       op0=ALU.mult, op1=ALU.subtract), reads=[ones8, fe, prev], writes=[cu])
                            c.dma("sp", cum_d[:, tsl], cu[:], reads=[cu])
                            for tt in range(2):
                                c.op("pe", lambda e: e.transpose(out=pf[:, 256 + tt * 8:256 + tt * 8 + 8], in_=cu[:, tt * 128:(tt + 1) * 128], identity=ident[0:8, 0:8]),
                                     reads=[cu, ident], writes=[pf])
                            c.op("dve", lambda e: e.tensor_scalar(out=ncT[:, ch * 2:ch * 2 + 2, :], in0=pf[:, 256:272].rearrange("p (a b) -> p a b", a=2),
                                                                  scalar1=-1.0, scalar2=None, op0=ALU.mult), reads=[pf], writes=[ncT])
                    c.barrier()

        if STOP == "p1":
            c.finish()
            return nc
        with contextlib.ExitStack() as ph:
            KT = c.sb("KT", [128, T], stack=ph)
            VA = c.sb("VA", [128, NB, 128], stack=ph)
            masks = c.sb("masks", [128, 4, 512], stack=ph)
            qts = [c.sb("qt%d" % i, [128, QC], stack=ph) for i in range(2)]
            cts = [c.sb("ct%d" % i, [128, QC], stack=ph) for i in range(2)]
            ctms = [c.sb("ctm%d" % i, [128, 4, QC], stack=ph) for i in range(1)]
            zts = [c.sb("zt%d" % i, [64, QC], stack=ph) for i in range(2)]
            tmps = [c.sb("tmp%d" % i, [128, QC], stack=ph) for i in range(3)]
            pTs = [c.sb("pT%d" % i, [128, QC], stack=ph) for i in range(3)]
            rd = c.sb("rd", [128, QC], stack=ph)
            ot = c.sb("ot", [64, QC], stack=ph)
            ez = c.sb("ez", [64, QC], stack=ph)
            ogs = [c.sb("og%d" % i, [64, QC], stack=ph) for i in range(2)]
            psS = [c.ps("psS%d" % i, [128, QC], stack=ph) for i in range(3)]
            pos = [c.ps("po%d" % i, [128, QC], stack=ph) for i in range(2)]
            c.dma("sp", masks[:], masks_d, writes=[masks])
            c.op("pool", lambda e: e.memset(VA[:, :, 64:128], 1.0), writes=[VA])
            pi = 0
            ci = 0
            for hp in range(4):
                nsp = max(1, T // 2048)
                for s in range(nsp):
                    w_ = T // nsp
                    c.dma("sp", KT[:, s * w_:(s + 1) * w_], kt_d[hp * 128:(hp + 1) * 128, s * w_:(s + 1) * w_], writes=[KT])
                for par in range(2):
                    h = hp * 2 + par
                    rows = slice(par * 64, par * 64 + 64)
                    vsrc = v_d[:, h * 64:(h + 1) * 64].rearrange("(b p) d -> p b d", p=128)
                    nvs = max(1, NB // 16)
                    for s in range(nvs):
                        b0 = s * (NB // nvs); b1 = (s + 1) * (NB // nvs)
                        c.dma("sp", VA[:, b0:b1, 0:64], vsrc[:, b0:b1, :], writes=[VA])
                    for cq in range(NQ):
                        qsl = slice(cq * QC, (cq + 1) * QC)
                        qt = qts[ci % 2]; ct = cts[ci % 2]; ctm = ctms[0]; zt = zts[ci % 2]
                        po = pos[ci % 2]; og = ogs[ci % 2]
                        ci += 1
                        c.dma("sp", qt[rows, :], qt_d[h * 64:(h + 1) * 64, qsl], writes=[qt])
                        c.dma("sp", ct[:].unsqueeze(1), bcast_rows(cum_d[h:h + 1, qsl]), writes=[ct])
                        c.dma("sp", zt[:], zt_d[h * 64:(h + 1) * 64, qsl], writes=[zt])
                        c.op("pool", lambda e: e.tensor_tensor(out=ctm[:], in0=masks[:], in1=ct[:].unsqueeze(1).to_broadcast([128, 4, QC]), op=ALU.add),
                             reads=[masks, ct], writes=[ctm])
                        nj = 4 * cq + 4
                        for j in range(nj):
                            ps_ = psS[pi % 3]; tmp = tmps[pi % 3]; pT = pTs[pi % 3]
                            pi += 1
                            c.op("pe", lambda e: e.matmul(ps_[:], KT[rows, j * 128:(j + 1) * 128], qt[rows, :], start=True, stop=True),
                                 reads=[KT, qt], writes=[ps_])
                            jj = j - 4 * cq
                            if jj < 0:
                                c.op("dve", lambda e: e.tensor_tensor(out=tmp[:], in0=ps_[:], in1=ct[:], op=ALU.add), reads=[ps_, ct], writes=[tmp])
                            else:
                                c.op("dve", lambda e: e.tensor_tensor(out=tmp[:], in0=ps_[:], in1=ctm[:, jj, :], op=ALU.add), reads=[ps_, ctm], writes=[tmp])
                            c.op("act", lambda e: e.activation(out=pT[:], in_=tmp[:], func=AF.Exp, scale=1.0, bias=ncT[:, j, h:h + 1]),
                                 reads=[tmp, ncT], writes=[pT])
                            c.op("pe", lambda e: e.matmul(po[:], VA[:, j, :], pT[:], start=(j == 0), stop=(j == nj - 1)),
                                 reads=[VA, pT], writes=[po])
                        c.op("dve", lambda e: e.reciprocal(out=rd[64:128, :], in_=po[64:128, :]), reads=[po], writes=[rd])
                        c.op("dve", lambda e: e.tensor_tensor(out=ot[:], in0=po[0:64, :], in1=rd[64:128, :], op=ALU.mult), reads=[po, rd], writes=[ot])
                        c.op("act", lambda e: e.activation(out=ez[:], in_=zt[:], func=AF.Exp, scale=-1.0), reads=[zt], writes=[ez])
                        c.op("pool", lambda e: e.tensor_scalar(out=ez[:], in0=ez[:], scalar1=1.0, scalar2=None, op0=ALU.add), reads=[ez], writes=[ez])
                        c.op("dve", lambda e: e.reciprocal(out=ez[:], in_=ez[:]), reads=[ez], writes=[ez])
                        c.op("pool", lambda e: e.tensor_tensor(out=ez[:], in0=ez[:], in1=zt[:], op=ALU.mult), reads=[ez, zt], writes=[ez])
                        c.op("dve", lambda e: e.tensor_tensor(out=og[:], in0=ot[:], in1=ez[:], op=ALU.mult), reads=[ot, ez], writes=[og])
                        c.dma("sp", ogt_d.sl(slice(h * 64, (h + 1) * 64), cq * QC, QC), og[:], reads=[og])
            c.barrier()
        if not fused:
            c.finish()
    return nc


def consts():
    ident = np.eye(128, dtype=np.float32)
    bones = np.zeros((128, 128), np.float32)
    bones[:64, :64] = 1.0
    bones[64:, 64:] = 1.0
    p = np.arange(128)[:, None]
    col = np.arange(512)[None, :]
    masks = np.zeros((128, 4, 512), np.float32)
    for jj in range(4):
        masks[:, jj, :] = np.where(col >= 128 * jj + p, 0.0, NEG)
    return ident, bones, masks


def fm(vec):
    return np.ascontiguousarray(vec.reshape(KC, 128).T)


def wpieces(w, ncols):
    n = w.shape[1] // ncols
    return np.ascontiguousarray(w.reshape(KC, 128, n, ncols).transpose(2, 1, 0, 3))


def inputs_A(inp, T):
    ident, bones, masks = consts()
    maps = []
    wmod = wpieces(inp["w_mod"][0], 512)
    win = inp["fox_w_in"][0]
    for core in range(8):
        b, g = core // 4, core % 4
        cols = []
        for typ in (0, 1, 3):
            for oc in range(4):
                c0 = typ * D + g * 512 + oc * 128
                cols.append(win[:, c0:c0 + 128])
        wqkz = np.ascontiguousarray(np.stack(cols, 0).reshape(12, KC, 128, 128).transpose(0, 2, 1, 3))
        wv = np.ascontiguousarray(win[:, 2 * D + g * 512:2 * D + (g + 1) * 512].reshape(KC, 128, 512).transpose(1, 0, 2))
        wf = np.ascontiguousarray(win[:, 4 * D + g * 8:4 * D + (g + 1) * 8].reshape(KC, 128, 8).transpose(1, 0, 2))
        qkw = np.stack([np.tile(inp["fox_q_norm_w"][0], 2), np.tile(inp["fox_k_norm_w"][0], 2)], 1).astype(np.float32)
        maps.append({
            "x": np.ascontiguousarray(inp["x"][b, :T]),
            "ca": fm(inp["c"][b]),
            "wmod": wmod,
            "bmod": np.ascontiguousarray(inp["b_mod"][0][None, :]),
            "normw": np.ascontiguousarray(inp["norm_w"][0][None, :]),
            "wqkz": wqkz, "wv": wv, "wf": wf,
            "bf": np.ascontiguousarray(inp["fox_b_f"][0][g * 8:(g + 1) * 8, None]),
            "qkw": np.ascontiguousarray(qkw),
            "ident": ident, "bones": bones, "masks": masks,
        })
    return maps


def run_A(inp, T):
    nc = build_A(T)
    res = run_bass_kernel_spmd(nc, inputs_A(inp, T), core_ids=list(range(8)))
    return res.results


def phase_resid(c, ntiles, xsrc, asrc, wout_d, gate_bc, store):
    with contextlib.ExitStack() as ph:
        wo = c.sb("wo", [128, KC, D], stack=ph)
        for k4 in range(4):
            c.dma("sp", wo[:, k4 * 4:(k4 + 1) * 4, :], wout_d[:, k4 * 4:(k4 + 1) * 4, :], writes=[wo])
        aT = [c.sb("aT%d" % i, [128, KC, 128], stack=ph) for i in range(2)]
        xt = [c.sb("xr%d" % i, [128, D], stack=ph) for i in range(2)]
        ot = [c.sb("or%d" % i, [128, D], stack=ph) for i in range(2)]
        pp = [c.ps("pr%d" % i, [128, 512], stack=ph) for i in range(4)]
        for i in range(ntiles):
            a = aT[i % 2]; x = xt[i % 2]; o = ot[i % 2]
            c.dma("sp", a[:], asrc(i), writes=[a])
            c.dma("sp", x[:], xsrc(i), writes=[x])
            for n in range(4):
                p = pp[n]
                for kc in range(KC):
                    c.op("pe", lambda e: e.matmul(p[:], a[:, kc, :], wo[:, kc, n * 512:(n + 1) * 512], start=(kc == 0), stop=(kc == KC - 1)),
                         reads=[a, wo], writes=[p])
                c.op("dve", lambda e: e.tensor_tensor(out=o[:, n * 512:(n + 1) * 512], in0=p[:], in1=gate_bc[:, n * 512:(n + 1) * 512], op=ALU.mult),
                     reads=[p, gate_bc], writes=[o])
            c.op("pool", lambda e: e.tensor_tensor(out=o[:], in0=o[:], in1=x[:], op=ALU.add), reads=[o, x], writes=[o])
            store(i, o)
        c.barrier()


def fm_rows(tw, tok0, n=128):
    return tw.sl(slice(None), tok0, n).rearrange("(kc p) t -> p kc t", p=128)


WDECAY = -0.6065306597126334


def build_B(T, fused=None):
    nc = fused["nc"] if fused else bass.Bass("TRN2", target_bir_lowering=False)
    NT = T // 128
    TQ = T // 4

    def din(name, shape):
        if fused:
            if name in fused["share"]:
                return fused["share"][name]
            return nc.dram_tensor("B_" + name, list(shape), F32, kind="ExternalInput").ap()
        return nc.dram_tensor(name, list(shape), F32, kind="ExternalInput").ap()

    def dscr(name, shape):
        if fused and name in fused["share"]:
            return fused["share"][name]
        return nc.dram_tensor(name, list(shape), F32, kind="Internal").ap()

    x_d = din("x", [T, D])
    ogt_d = fused["share"]["ogt"] if fused else TokWhole(din("ogt", [D, T]))
    ca_d = din("ca", [128, KC])
    wmod0_d = din("wmod0g", [4, 128, KC, 512])
    bmod0_d = din("bmod0", [1, 6144])
    wmod1_d = din("wmod1", [8, 128, KC, 512])
    bmod1_d = din("bmod1", [1, 6144])
    normw_d = din("normw", [1, D])
    wout_d = din("wout", [128, KC, D])
    mu_d = din("mu", [128, 6, KC])
    w4_d = din("w4", [4, 2, 128, KC, 256])
    l1_d = din("l1", [128, KC, 256])
    w2_d = din("w2", [96, 512])
    a2_d = din("a2", [96, 512])
    v2_d = din("v2", [64, 512])
    pvec_d = din("pvec", [6, 512])
    lnx_d = din("lnx", [128, 2, 4])
    vf_d = din("vfirst", [T, 512])
    ident_d = din("ident", [128, 128])
    bones_d = din("bones", [128, 128])
    sel_d = din("sel", [32, 16, 128])
    if fused:
        ygt_d = fused["share"]["B_ygt"]
    else:
        ygt_d = TokWhole(nc.dram_tensor("ygt", [512, T], F32, kind="ExternalOutput").ap())
    X1 = dscr("x1_s", [T, D])
    R5 = dscr("r5_s", [T, 2, 5, 4, 64])
    VT = dscr("vt_s", [512, T])
    BVT = dscr("bvt_s", [512, T])
    SZT = dscr("szt_s", [512, T])
    YT = dscr("yt_s", [512, T])

    with contextlib.ExitStack() as st:
        c = fused["c"] if fused else Ctx(nc, st)
        ident = c.sb("ident", [128, 128], stack=st)
        bones = c.sb("bones", [128, 128], stack=st)
        cst = c.sb("cst", [128, 4], stack=st)
        c.dma("sp", ident[:], ident_d, writes=[ident])
        c.dma("sp", bones[:], bones_d, writes=[bones])
        c.op("dve", lambda e: e.memset(cst[:, 0:1], NORM_EPS), writes=[cst])
        c.op("dve", lambda e: e.memset(cst[:, 1:2], LNX_EPS), writes=[cst])
        c.op("dve", lambda e: e.memset(cst[:, 2:3], 0.0), writes=[cst])

        if "b1" not in SKIP:
            with contextlib.ExitStack() as g0:
                gate_bc = c.sb("gate0", [128, D], stack=g0)
                phase_mod(c, ca_d, lambda pc: wmod0_d[pc - 8], bmod0_d, lambda pc: (gate_bc, (pc - 8) * 512), [8, 9, 10, 11])
                phase_resid(c, NT, lambda i: x_d[i * 128:(i + 1) * 128, :], lambda i: fm_rows(ogt_d, i * 128), wout_d, gate_bc,
                            lambda i, o: c.dma("sp", X1[i * 128:(i + 1) * 128, :], o[:], reads=[o]))
        if STOP == "b1":
            c.finish()
            return nc

        with contextlib.ExitStack() as g1:
            shift_bc = c.sb("shift1", [128, D], stack=g1)
            A_bc = c.sb("A1", [128, D], stack=g1)
            phase_mod(c, ca_d, wmod1_d, bmod1_d,
                      lambda pc: (shift_bc, pc * 512) if pc < 4 else (A_bc, (pc - 4) * 512), list(range(8)))
            make_A(c, normw_d, A_bc)
            with contextlib.ExitStack() as ph:
                def T2(name):
                    return c.sb(name, [128, 512], stack=ph)
                mu = c.sb("mu", [128, 6, KC], stack=ph)
                l1 = c.sb("l1", [128, KC, 256], stack=ph)
                w2 = c.sb("w2", [96, 512], stack=ph)
                a2 = c.sb("a2", [96, 512], stack=ph)
                v2 = c.sb("v2", [64, 512], stack=ph)
                pv_ = [T2("pvec%d" % i) for i in range(6)]
                omk = T2("omk")
                wb = [c.sb("wb%d" % i, [128, KC, 256], stack=ph) for i in range(3)]
                xt = c.sb("xt", [128, D], stack=ph)
                hm = c.sb("hm", [128, D], stack=ph)
                ss = c.sb("ss", [128, 1], stack=ph)
                sd = c.sb("sd", [128, 1], stack=ph)
                rstd = c.sb("rstd", [128, 1], stack=ph)
                hT = c.sb("hT", [128, KC, 129], stack=ph)
                xx = c.sb("xx", [128, KC, 128], stack=ph)
                mixs = [c.sb("mix%d" % i, [128, KC, 128], stack=ph) for i in range(2)]
                r_sb = T2("r_sb"); w_sb = T2("w_sb"); k_sb = T2("k_sb"); a_sb = T2("a_sb"); kk = T2("kk")
                av = T2("av"); bv_ = T2("bv"); tq = T2("tq"); v_sb = T2("v_sb"); vg = T2("vg"); vf = T2("vf")
                vT_sb = c.sb("vT_sb", [128, 4, 128], stack=ph)
                bT_sb = c.sb("bT_sb", [128, 4, 128], stack=ph)
                zs = c.sb("zs", [128, 4, 128], stack=ph)
                lo = [c.sb("lo%d" % i, [96, 128], stack=ph) for i in range(3)]
                s8 = c.sb("s8", [128, 8], stack=ph)
                r8 = c.sb("r8", [128, 8], stack=ph)
                bon = c.sb("bon", [128, 8], stack=ph)
                ptr = [c.ps("ptr%d" % i, [128, 4, 128], stack=ph) for i in range(2)]
                pb = [c.ps("pb%d" % i, [128, 512], stack=ph) for i in range(3)]
                pl = c.ps("pl", [128, 512], stack=ph)
                pz = c.ps("pz", [128, 4, 128], stack=ph)
                c.dma("sp", mu[:], mu_d, writes=[mu])
                c.dma("sp", l1[:], l1_d, writes=[l1])
                c.dma("sp", w2[:], w2_d, writes=[w2])
                c.dma("sp", a2[:], a2_d, writes=[a2])
                c.dma("sp", v2[:], v2_d, writes=[v2])
                for i in range(6):
                    c.dma("sp", pv_[i][:].unsqueeze(1), bcast_rows(pvec_d[i:i + 1, :]), writes=[pv_[i]])
                w0b, a0b, v0b, kkb, kab, rkb = pv_
                c.op("dve", lambda e: e.tensor_scalar(out=omk[:], in0=kab[:], scalar1=-1.0, scalar2=1.0, op0=ALU.mult, op1=ALU.add), reads=[kab], writes=[omk])
                c.op("dve", lambda e: e.memset(hT[:, :, 0:1], 0.0), writes=[hT])
                tcnt = [0]
                wi = [0]
                pbi = [0]

                def proj_tm(mix, widx):
                    p = pb[pbi[0] % 3]
                    pbi[0] += 1
                    for half in range(2):
                        w = wb[wi[0] % 3]
                        wi[0] += 1
                        c.dma("sp", w[:], w4_d[widx, half], writes=[w])
                        for kc in range(KC):
                            c.op("pe", lambda e: e.matmul(p[:, half * 256:(half + 1) * 256], mix[:, kc, :], w[:, kc, :], start=(kc == 0), stop=(kc == KC - 1)),
                                 reads=[mix, w], writes=[p])
                    return p

                def lora_tm(mix, c0, n, act, w2t, lot):
                    for kc in range(KC):
                        c.op("pe", lambda e: e.matmul(pl[0:n, 0:128], l1[:, kc, c0:c0 + n], mix[:, kc, :], start=(kc == 0), stop=(kc == KC - 1)),
                             reads=[l1, mix], writes=[pl])
                    c.op("act", lambda e: e.activation(out=lot[0:n, :], in_=pl[0:n, 0:128], func=act), reads=[pl], writes=[lot])
                    p = pb[pbi[0] % 3]
                    pbi[0] += 1
                    c.op("pe", lambda e: e.matmul(p[:], lot[0:n, :], w2t[0:n, :], start=True, stop=True), reads=[lot, w2t], writes=[p])
                    return p

                def store_r5(tile, q, tok0):
                    tv = tile[:].rearrange("p (hg par k) -> p hg par k", hg=4, par=2)
                    for par in range(2):
                        c.dma("sp", R5[tok0:tok0 + 128, par, q, :, :], tv[:, :, par, :], reads=[tile])

                def v3(t_):
                    return t_[:].rearrange("p (h k) -> p h k", h=8)

                for it in range(NT):
                    tok0 = it * 128
                    norm_tile(c, X1[tok0:tok0 + 128, :], xt, ss, sd, rstd, hm, A_bc, shift_bc)
                    for g4 in range(4):
                        p = ptr[tcnt[0] % 2]
                        tcnt[0] += 1
                        for j in range(4):
                            kc = g4 * 4 + j
                            c.op("pe", lambda e: e.transpose(out=p[:, j, :], in_=hm[:, kc * 128:(kc + 1) * 128], identity=ident[:]),
                                 reads=[hm, ident], writes=[p])
                        c.op("act", lambda e: e.activation(out=hT[:, g4 * 4:(g4 + 1) * 4, 1:129], in_=p[:], func=AF.Copy), reads=[p], writes=[hT])
                    c.op("dve", lambda e: e.tensor_tensor(out=xx[:], in0=hT[:, :, 0:128], in1=hT[:, :, 1:129], op=ALU.subtract), reads=[hT], writes=[xx])

                    def mk_mix(q):
                        m = mixs[q % 2]
                        c.op("dve", lambda e: e.tensor_tensor(out=m[:], in0=xx[:], in1=mu[:, q, :].unsqueeze(2).to_broadcast([128, KC, 128]), op=ALU.mult),
                             reads=[xx, mu], writes=[m])
                        c.op("pool", lambda e: e.tensor_tensor(out=m[:], in0=m[:], in1=hT[:, :, 1:129], op=ALU.add), reads=[m, hT], writes=[m])
                        return m

                    m = mk_mix(0)
                    p = proj_tm(m, 0)
                    c.op("act", lambda e: e.activation(out=r_sb[:], in_=p[:], func=AF.Copy), reads=[p], writes=[r_sb])
                    m = mk_mix(1)
                    p = lora_tm(m, 0, 96, AF.Tanh, w2, lo[0])
                    c.op("dve", lambda e: e.tensor_tensor(out=w_sb[:], in0=p[:], in1=w0b[:], op=ALU.add), reads=[p, w0b], writes=[w_sb])
                    c.op("act", lambda e: e.activation(out=w_sb[:], in_=w_sb[:], func=AF.Sigmoid), reads=[w_sb], writes=[w_sb])
                    c.op("act", lambda e: e.activation(out=w_sb[:], in_=w_sb[:], func=AF.Exp, scale=WDECAY), reads=[w_sb], writes=[w_sb])
                    m = mk_mix(2)
                    p = proj_tm(m, 1)
                    c.op("act", lambda e: e.activation(out=k_sb[:], in_=p[:], func=AF.Copy), reads=[p], writes=[k_sb])
                    m = mk_mix(3)
                    p = proj_tm(m, 2)
                    c.op("act", lambda e: e.activation(out=v_sb[:], in_=p[:], func=AF.Copy), reads=[p], writes=[v_sb])
                    p = lora_tm(m, 192, 64, AF.Copy, v2, lo[1])
                    c.op("dve", lambda e: e.tensor_tensor(out=vg[:], in0=p[:], in1=v0b[:], op=ALU.add), reads=[p, v0b], writes=[vg])
                    c.op("act", lambda e: e.activation(out=vg[:], in_=vg[:], func=AF.Sigmoid), reads=[vg], writes=[vg])
                    m = mk_mix(4)
                    p = lora_tm(m, 96, 96, AF.Copy, a2, lo[2])
                    c.op("dve", lambda e: e.tensor_tensor(out=a_sb[:], in0=p[:], in1=a0b[:], op=ALU.add), reads=[p, a0b], writes=[a_sb])
                    c.op("act", lambda e: e.activation(out=a_sb[:], in_=a_sb[:], func=AF.Sigmoid), reads=[a_sb], writes=[a_sb])
                    m = mk_mix(5)
                    for oc in range(4):
                        half = oc // 2
                        if oc % 2 == 0:
                            w = wb[wi[0] % 3]
                            wi[0] += 1
                            c.dma("sp", w[:], w4_d[3, half], writes=[w])
                        for kc in range(KC):
                            c.op("pe", lambda e: e.matmul(pz[:, oc, :], w[:, kc, (oc % 2) * 128:(oc % 2) * 128 + 128], m[:, kc, :], start=(kc == 0), stop=(kc == KC - 1)),
                                 reads=[w, m], writes=[pz])
                    c.op("act", lambda e: e.activation(out=zs[:], in_=pz[:], func=AF.Silu), reads=[pz], writes=[zs])
                    c.dma("sp", SZT[:, tok0:tok0 + 128].rearrange("(hg p) t -> p hg t", p=128), zs[:], reads=[zs])
                    c.op("pool", lambda e: e.tensor_copy(out=hT[:, :, 0:1], in_=hT[:, :, 128:129]), reads=[hT], writes=[hT])

                    c.op("dve", lambda e: e.tensor_tensor(out=kk[:], in0=k_sb[:], in1=kkb[:], op=ALU.mult), reads=[k_sb, kkb], writes=[kk])
                    c.op("pool", lambda e: e.tensor_tensor(out=tq[:], in0=kk[:], in1=kk[:], op=ALU.mult), reads=[kk], writes=[tq])
                    c.op("dve", lambda e: e.tensor_reduce(out=s8[:], in_=v3(tq), axis=AX.X, op=ALU.add), reads=[tq], writes=[s8])
                    c.op("act", lambda e: e.activation(out=s8[:], in_=s8[:], func=AF.Sqrt), reads=[s8], writes=[s8])
                    c.op("dve", lambda e: e.tensor_scalar(out=s8[:], in0=s8[:], scalar1=1e-12, scalar2=None, op0=ALU.max), reads=[s8], writes=[s8])
                    c.op("dve", lambda e: e.reciprocal(out=r8[:], in_=s8[:]), reads=[s8], writes=[r8])
                    c.op("dve", lambda e: e.tensor_tensor(out=v3(kk), in0=v3(kk), in1=r8[:].unsqueeze(2).to_broadcast([128, 8, 64]), op=ALU.mult),
                         reads=[kk, r8], writes=[kk])
                    c.op("pool", lambda e: e.tensor_scalar(out=av[:], in0=kk[:], scalar1=-1.0, scalar2=None, op0=ALU.mult), reads=[kk], writes=[av])
                    c.op("pool", lambda e: e.tensor_tensor(out=bv_[:], in0=kk[:], in1=a_sb[:], op=ALU.mult), reads=[kk, a_sb], writes=[bv_])
                    c.op("pool", lambda e: e.tensor_tensor(out=tq[:], in0=a_sb[:], in1=kab[:], op=ALU.mult), reads=[a_sb, kab], writes=[tq])
                    c.op("pool", lambda e: e.tensor_tensor(out=tq[:], in0=tq[:], in1=omk[:], op=ALU.add), reads=[tq, omk], writes=[tq])
                    c.op("dve", lambda e: e.tensor_tensor(out=k_sb[:], in0=k_sb[:], in1=tq[:], op=ALU.mult), reads=[k_sb, tq], writes=[k_sb])
                    c.op("dve", lambda e: e.tensor_tensor(out=tq[:], in0=r_sb[:], in1=k_sb[:], op=ALU.mult), reads=[r_sb, k_sb], writes=[tq])
                    c.op("pool", lambda e: e.tensor_tensor(out=tq[:], in0=tq[:], in1=rkb[:], op=ALU.mult), reads=[tq, rkb], writes=[tq])
                    c.op("dve", lambda e: e.tensor_reduce(out=bon[:], in_=v3(tq), axis=AX.X, op=ALU.add), reads=[tq], writes=[bon])
                    c.dma("sp", vf[:], vf_d[tok0:tok0 + 128, :], writes=[vf])
                    c.op("pool", lambda e: e.tensor_tensor(out=vf[:], in0=vf[:], in1=v_sb[:], op=ALU.subtract), reads=[vf, v_sb], writes=[vf])
                    c.op("dve", lambda e: e.tensor_tensor(out=vf[:], in0=vf[:], in1=vg[:], op=ALU.mult), reads=[vf, vg], writes=[vf])
                    c.op("pool", lambda e: e.tensor_tensor(out=v_sb[:], in0=v_sb[:], in1=vf[:], op=ALU.add), reads=[v_sb, vf], writes=[v_sb])
                    c.op("dve", lambda e: e.tensor_tensor(out=v3(vf), in0=v3(v_sb), in1=bon[:].unsqueeze(2).to_broadcast([128, 8, 64]), op=ALU.mult),
                         reads=[v_sb, bon], writes=[vf])
                    for src, dst_sb, dscr_ in ((v_sb, vT_sb, VT), (vf, bT_sb, BVT)):
                        p = ptr[tcnt[0] % 2]
                        tcnt[0] += 1
                        for hg in range(4):
                            c.op("pe", lambda e: e.transpose(out=p[:, hg, :], in_=src[:, hg * 128:(hg + 1) * 128], identity=ident[:]),
                                 reads=[src, ident], writes=[p])
                        c.op("act", lambda e: e.activation(out=dst_sb[:], in_=p[:], func=AF.Copy), reads=[p], writes=[dst_sb])
                        c.dma("sp", dscr_[:, tok0:tok0 + 128].rearrange("(hg p) t -> p hg t", p=128), dst_sb[:], reads=[dst_sb])
                    store_r5(r_sb, 0, tok0)
                    store_r5(w_sb, 1, tok0)
                    store_r5(k_sb, 2, tok0)
                    store_r5(av, 3, tok0)
                    store_r5(bv_, 4, tok0)
                c.barrier()
        if STOP == "b2":
            c.finish()
            return nc

        with contextlib.ExitStack() as ph:
            sel = c.sb("sel", [32, 16, 128], stack=ph)
            rows = [c.sb("rows%d" % i, [32, 1280], stack=ph) for i in range(3)]
            vts = [c.sb("vts%d" % i, [128, 4, 512], stack=ph) for i in range(2)]
            yts = [c.sb("yts%d" % i, [128, 4, 512], stack=ph) for i in range(2)]
            S = c.sb("S", [128, 4, 64], stack=ph)
            tmp = c.sb("tmpS", [128, 4, 64], stack=ph)
            tmpK = c.sb("tmpK", [128, 4, 64], stack=ph)
            tmpB = c.sb("tmpB", [128, 4, 64], stack=ph)
            tmpRs = [c.sb("tmpR%d" % i, [128, 4, 64], stack=ph) for i in range(2)]
            pend = []
            sa = c.sb("sa", [128, 4], stack=ph)
            bcs = [c.ps("bc%d" % i, [128, 3, 512], stack=ph) for i in range(2)]
            c.dma("sp", sel[:], sel_d, writes=[sel])
            c.op("dve", lambda e: e.memset(S[:], 0.0), writes=[S])
            BL = min(512, T)
            for t in range(T):
                if t % BL == 0:
                    vt = vts[(t // BL) % 2]
                    yt = yts[(t // BL) % 2]
                    c.dma("sp", vt[:, :, 0:BL], VT[:, t:t + BL].rearrange("(hg p) t -> p hg t", p=128), writes=[vt])
                if t % 16 == 0:
                    rw = rows[(t // 16) % 3]
                    c.dma("sp", rw[:], R5[t:t + 16].rearrange("t par q hg k -> (t par) (q hg k)"), writes=[rw])
                tb = t % 16
                tl = t % BL
                bc = bcs[t % 2]
                for part, (c0, n) in enumerate(((0, 512), (512, 512), (1024, 256))):
                    c.op("pe", lambda e: e.matmul(bc[:, part, 0:n], sel[:, tb, :], rw[:, c0:c0 + n], start=True, stop=True),
                         reads=[sel, rw], writes=[bc])

                def bq(q):
                    pq_, of_ = [(0, 0), (0, 256), (1, 0), (1, 256), (2, 0)][q]
                    return bc[:, pq_, of_:of_ + 256].rearrange("p (a b) -> p a b", a=4)
                c.op("dve", lambda e: e.tensor_tensor(out=tmp[:], in0=S[:], in1=bq(3), op=ALU.mult), reads=[S, bc], writes=[tmp])
                c.op("dve", lambda e: e.tensor_tensor(out=tmpK[:], in0=bq(2), in1=vt[:, :, tl:tl + 1].to_broadcast([128, 4, 64]), op=ALU.mult),
                     reads=[bc, vt], writes=[tmpK])
                if pend:
                    pend.pop()()
                c.op("dve", lambda e: e.tensor_reduce(out=sa[:], in_=tmp[:], axis=AX.X, op=ALU.add), reads=[tmp], writes=[sa])
                c.op("dve", lambda e: e.tensor_tensor(out=S[:], in0=S[:], in1=bq(1), op=ALU.mult), reads=[S, bc], writes=[S])
                c.op("dve", lambda e: e.tensor_tensor(out=tmpB[:], in0=bq(4), in1=sa[:].unsqueeze(2).to_broadcast([128, 4, 64]), op=ALU.mult),
                     reads=[bc, sa], writes=[tmpB])
                c.op("dve", lambda e: e.tensor_tensor(out=S[:], in0=S[:], in1=tmpK[:], op=ALU.add), reads=[S, tmpK], writes=[S])
                c.op("dve", lambda e: e.tensor_tensor(out=S[:], in0=S[:], in1=tmpB[:], op=ALU.add), reads=[S, tmpB], writes=[S])
                tr = tmpRs[t % 2]
                c.op("dve", lambda e: e.tensor_tensor(out=tr[:], in0=S[:], in1=bq(0), op=ALU.mult), reads=[S, bc], writes=[tr])

                def fin(tr=tr, yt=yt, tl=tl, t=t):
                    c.op("dve", lambda e: e.tensor_reduce(out=yt[:, :, tl:tl + 1], in_=tr[:], axis=AX.X, op=ALU.add), reads=[tr], writes=[yt])
                    if tl == BL - 1:
                        c.dma("sp", YT[:, t + 1 - BL:t + 1].rearrange("(hg p) t -> p hg t", p=128), yt[:, :, 0:BL], reads=[yt])
                pend.append(fin)
            while pend:
                pend.pop()()
            c.barrier()
        if STOP == "b3":
            c.finish()
            return nc

        with contextlib.ExitStack() as ph:
            lnx = c.sb("lnx", [128, 2, 4], stack=ph)
            c.dma("sp", lnx[:], lnx_d, writes=[lnx])
            ys = [c.sb("y%d" % i, [128, 512], stack=ph) for i in range(2)]
            bvs = [c.sb("bvl%d" % i, [128, 512], stack=ph) for i in range(2)]
            szs = [c.sb("szl%d" % i, [128, 512], stack=ph) for i in range(2)]
            dd = c.sb("dd", [128, 512], stack=ph)
            d2 = c.sb("d2", [128, 512], stack=ph)
            rs = c.sb("rs", [128, 512], stack=ph)
            ygs = [c.sb("yg%d" % i, [128, 512], stack=ph) for i in range(2)]
            pm = c.ps("pmean", [128, 512], stack=ph)
            pvr = c.ps("pvar", [128, 512], stack=ph)
            BL = min(512, T)
            i = 0
            for t0 in range(0, T, BL):
                for hg in range(4):
                    y = ys[i % 2]; bvl = bvs[i % 2]; szl = szs[i % 2]; yg = ygs[i % 2]
                    i += 1
                    rsl = slice(hg * 128, (hg + 1) * 128)
                    c.dma("sp", y[:, 0:BL], YT[rsl, t0:t0 + BL], writes=[y])
                    c.dma("sp", bvl[:, 0:BL], BVT[rsl, t0:t0 + BL], writes=[bvl])
                    c.dma("sp", szl[:, 0:BL], SZT[rsl, t0:t0 + BL], writes=[szl])
                    c.op("pe", lambda e: e.matmul(pm[:, 0:BL], bones[:], y[:, 0:BL], start=True, stop=True), reads=[bones, y], writes=[pm])
                    c.op("dve", lambda e: e.scalar_tensor_tensor(out=dd[:, 0:BL], in0=pm[:, 0:BL], scalar=-1.0 / 64.0, in1=y[:, 0:BL], op0=ALU.mult, op1=ALU.add),
                         reads=[pm, y], writes=[dd])
                    c.op("act", lambda e: e.activation(out=d2[:, 0:BL], in_=dd[:, 0:BL], func=AF.Square), reads=[dd], writes=[d2])
                    c.op("pe", lambda e: e.matmul(pvr[:, 0:BL], bones[:], d2[:, 0:BL], start=True, stop=True), reads=[bones, d2], writes=[pvr])
                    c.op("act", lambda e: e.activation(out=rs[:, 0:BL], in_=pvr[:, 0:BL], func=AF.Sqrt, scale=1.0 / 64.0, bias=cst[:, 1:2]), reads=[pvr, cst], writes=[rs])
                    c.op("dve", lambda e: e.reciprocal(out=rs[:, 0:BL], in_=rs[:, 0:BL]), reads=[rs], writes=[rs])
                    c.op("dve", lambda e: e.tensor_tensor(out=dd[:, 0:BL], in0=dd[:, 0:BL], in1=rs[:, 0:BL], op=ALU.mult), reads=[dd, rs], writes=[dd])
                    c.op("act", lambda e: e.activation(out=dd[:, 0:BL], in_=dd[:, 0:BL], func=AF.Identity, scale=lnx[:, 0, hg:hg + 1], bias=lnx[:, 1, hg:hg + 1]),
                         reads=[dd, lnx], writes=[dd])
                    c.op("pool", lambda e: e.tensor_tensor(out=dd[:, 0:BL], in0=dd[:, 0:BL], in1=bvl[:, 0:BL], op=ALU.add), reads=[dd, bvl], writes=[dd])
                    c.op("dve", lambda e: e.tensor_tensor(out=yg[:, 0:BL], in0=dd[:, 0:BL], in1=szl[:, 0:BL], op=ALU.mult), reads=[dd, szl], writes=[yg])
                    c.dma("sp", ygt_d.sl(rsl, t0, BL), yg[:, 0:BL], reads=[yg])
            c.barrier()
        if not fused:
            c.finish()
    return nc


def build_C(TQ):
    nc = bass.Bass("TRN2", target_bir_lowering=False)
    NT = TQ // 128

    def din(name, shape):
        return nc.dram_tensor(name, list(shape), F32, kind="ExternalInput").ap()

    x_d = din("x", [TQ, D])
    ogt_d = TokWhole(din("ogt", [D, TQ]))
    ygt_d = TokWhole(din("ygt", [D, TQ]))
    ca_d = din("ca", [128, KC])
    wmod0_d = din("wmod0g", [4, 128, KC, 512])
    bmod0_d = din("bmod0", [1, 6144])
    wmod1_d = din("wmod1g", [4, 128, KC, 512])
    bmod1_d = din("bmod1", [1, 6144])
    wout0_d = din("wout0", [128, KC, D])
    wout1_d = din("wout1", [128, KC, D])
    out_d = nc.dram_tensor("out", [TQ, D], F32, kind="ExternalOutput").ap()
    X1 = nc.dram_tensor("x1_s", [TQ, D], F32, kind="Internal").ap()
    with contextlib.ExitStack() as st:
        c = Ctx(nc, st)
        for (wm, bm, src, a_d, wo, dst) in ((wmod0_d, bmod0_d, x_d, ogt_d, wout0_d, X1), (wmod1_d, bmod1_d, X1, ygt_d, wout1_d, out_d)):
            with contextlib.ExitStack() as g0:
                gate_bc = c.sb("gate", [128, D], stack=g0)
                phase_mod(c, ca_d, lambda pc: wm[pc - 8], bm, lambda pc: (gate_bc, (pc - 8) * 512), [8, 9, 10, 11])
                phase_resid(c, NT, lambda i: src[i * 128:(i + 1) * 128, :], lambda i: fm_rows(a_d, i * 128), wo, gate_bc,
                            lambda i, o: c.dma("sp", dst[i * 128:(i + 1) * 128, :], o[:], reads=[o]))
        c.finish()
    return nc


def sel_const():
    sel = np.zeros((32, 16, 128), np.float32)
    for tb in range(16):
        for par in range(2):
            sel[tb * 2 + par, tb, par * 64:(par + 1) * 64] = 1.0
    return sel


def kcp(w):
    return np.ascontiguousarray(w.reshape(KC, 128, w.shape[1]).transpose(1, 0, 2))


def inputs_B(inp, T, ogt_full, v_cores):
    ident, bones, _ = consts()
    sel = sel_const()
    wmod0g = np.ascontiguousarray(wpieces(inp["w_mod"][0][:, 2 * D:], 512))
    wmod1 = np.ascontiguousarray(wpieces(inp["w_mod"][1][:, :2 * D], 512))
    wout = kcp(inp["fox_w_out"][0])
    mu = np.ascontiguousarray(inp["rwkv_mu"][0].reshape(6, KC, 128).transpose(2, 0, 1))
    l1 = kcp(np.concatenate([inp["rwkv_w1"][0], inp["rwkv_a1"][0], inp["rwkv_v1"][0]], axis=1))
    maps = []
    for core in range(8):
        b, g = core // 4, core % 4
        cs = slice(g * 512, (g + 1) * 512)
        w4 = np.stack([np.ascontiguousarray(inp["rwkv_w_in"][0][i][:, cs].reshape(KC, 128, 2, 256).transpose(2, 1, 0, 3)) for i in range(4)], 0)
        pvec = np.stack([inp["rwkv_w0"][0][cs], inp["rwkv_a0"][0][cs], inp["rwkv_v0"][0][cs], inp["rwkv_k_k"][0][cs],
                         inp["rwkv_k_a"][0][cs], inp["rwkv_r_k"][0].reshape(-1)[cs]], 0)
        lnx = np.stack([inp["rwkv_lnx_w"][0][cs].reshape(4, 128).T, inp["rwkv_lnx_b"][0][cs].reshape(4, 128).T], 1)
        maps.append({
            "x": np.ascontiguousarray(inp["x"][b, :T]),
            "ogt": ogt_full[b],
            "ca": fm(inp["c"][b]),
            "wmod0g": wmod0g, "bmod0": np.ascontiguousarray(inp["b_mod"][0][None, :]),
            "wmod1": wmod1, "bmod1": np.ascontiguousarray(inp["b_mod"][1][None, :]),
            "normw": np.ascontiguousarray(inp["norm_w"][1][None, :]),
            "wout": wout, "mu": mu, "w4": np.ascontiguousarray(w4), "l1": l1,
            "w2": np.ascontiguousarray(inp["rwkv_w2"][0][:, cs]), "a2": np.ascontiguousarray(inp["rwkv_a2"][0][:, cs]),
            "v2": np.ascontiguousarray(inp["rwkv_v2"][0][:, cs]),
            "pvec": np.ascontiguousarray(pvec.astype(np.float32)), "lnx": np.ascontiguousarray(lnx.astype(np.float32)),
            "vfirst": v_cores[core], "ident": ident, "bones": bones, "sel": sel,
        })
    return maps


def inputs_C(inp, T, ogt_full, ygt_full):
    TQ = T // 4
    wmod0g = np.ascontiguousarray(wpieces(inp["w_mod"][0][:, 2 * D:], 512))
    wmod1g = np.ascontiguousarray(wpieces(inp["w_mod"][1][:, 2 * D:], 512))
    wout0 = kcp(inp["fox_w_out"][0])
    wout1 = kcp(inp["rwkv_w_out"][0])
    maps = []
    for core in range(8):
        b, g = core // 4, core % 4
        ts = slice(g * TQ, (g + 1) * TQ)
        maps.append({
            "x": np.ascontiguousarray(inp["x"][b, ts]),
            "ogt": np.ascontiguousarray(ogt_full[b][:, ts]),
            "ygt": np.ascontiguousarray(ygt_full[b][:, ts]),
            "ca": fm(inp["c"][b]),
            "wmod0g": wmod0g, "bmod0": np.ascontiguousarray(inp["b_mod"][0][None, :]),
            "wmod1g": wmod1g, "bmod1": np.ascontiguousarray(inp["b_mod"][1][None, :]),
            "wout0": wout0, "wout1": wout1,
        })
    return maps


def run_all(inp, T):
    inp = {k: np.asarray(v, dtype=np.float32) for k, v in inp.items()}
    resA = run_bass_kernel_spmd(build_A(T), inputs_A(inp, T), core_ids=list(range(8))).results
    ogt_full = [np.ascontiguousarray(np.concatenate([resA[b * 4 + g]["ogt"] for g in range(4)], 0)) for b in range(2)]
    v_cores = [np.ascontiguousarray(resA[core]["v"]) for core in range(8)]
    resB = run_bass_kernel_spmd(build_B(T), inputs_B(inp, T, ogt_full, v_cores), core_ids=list(range(8))).results
    ygt_full = [np.ascontiguousarray(np.concatenate([resB[b * 4 + g]["ygt"] for g in range(4)], 0)) for b in range(2)]
    resC = run_bass_kernel_spmd(build_C(T // 4), inputs_C(inp, T, ogt_full, ygt_full), core_ids=list(range(8))).results
    TQ = T // 4
    out = np.zeros((2, T, D), np.float32)
    for core in range(8):
        b, g = core // 4, core % 4
        out[b, g * TQ:(g + 1) * TQ] = resC[core]["out"]
    return out


def build_F(T):
    nc = bass.Bass("TRN2", target_bir_lowering=False)
    NT = T // 128

    def ext(name, shape):
        return nc.dram_tensor(name, list(shape), F32, kind="ExternalInput").ap()

    def scr(name, shape):
        return nc.dram_tensor(name, list(shape), F32, kind="Internal").ap()

    share = {}
    share["x"] = ext("x", [T, D])
    share["ca"] = ext("ca", [128, KC])
    share["ident"] = ext("ident", [128, 128])
    share["bones"] = ext("bones", [128, 128])
    wmod0 = ext("wmod0", [12, 128, KC, 512])
    wmod1 = ext("wmod1", [12, 128, KC, 512])
    share["wmod"] = wmod0
    share["wmod0g"] = wmod0[8:12]
    share["wmod1"] = wmod1
    share["bmod"] = ext("bmod0", [1, 6144])
    share["bmod0"] = share["bmod"]
    share["bmod1"] = ext("bmod1", [1, 6144])
    wout1_d = ext("wout1", [128, KC, D])
    NP = T // 512
    og_own = [scr("ogo%d" % i, [512, 512]) for i in range(NP)]
    og_full = [scr("ogf%d" % i, [D, 512]) for i in range(NP)]
    yg_own = [scr("ygo%d" % i, [512, 512]) for i in range(NP)]
    yg_full = [scr("ygf%d" % i, [D, 512]) for i in range(NP)]
    share["A_ogt"] = TokPieces(og_own)
    share["A_v"] = scr("v_own", [T, 512])
    share["ogt"] = TokPieces(og_full)
    share["vfirst"] = share["A_v"]
    share["B_ygt"] = TokPieces(yg_own)
    share["x1_s"] = scr("x1_s", [T, D])
    ygt_full = TokPieces(yg_full)
    out_d = nc.dram_tensor("out", [T, D], F32, kind="ExternalOutput").ap()
    groups = [[0, 1, 2, 3], [4, 5, 6, 7]]
    with contextlib.ExitStack() as st:
        c = Ctx(nc, st)
        fused = dict(nc=nc, c=c, share=share)
        ccsem = st.enter_context(nc.semaphore("ccsem"))

        ccn = [0]

        def allgather(srcs, dsts):
            for src, dst in zip(srcs, dsts):
                nc.gpsimd.collective_compute("AllGather", ALU.bypass, replica_groups=groups, ins=[src], outs=[dst]).then_inc(ccsem, 1)
                ccn[0] += 1
            nc.gpsimd.wait_ge(ccsem, ccn[0])
            c.barrier()

        build_A(T, fused)
        allgather(og_own, og_full)
        build_B(T, fused)
        allgather(yg_own, yg_full)
        X1 = share["x1_s"]
        with contextlib.ExitStack() as g0:
            gate_bc = c.sb("gate1", [128, D], stack=g0)
            phase_mod(c, share["ca"], lambda pc: wmod1[pc], share["bmod1"], lambda pc: (gate_bc, (pc - 8) * 512), [8, 9, 10, 11])
            phase_resid(c, NT, lambda i: X1[i * 128:(i + 1) * 128, :], lambda i: fm_rows(ygt_full, i * 128), wout1_d, gate_bc,
                        lambda i, o: c.dma("sp", out_d[i * 128:(i + 1) * 128, :], o[:], reads=[o]))
        c.finish()
    return nc


def inputs_F(inp, T):
    mA = inputs_A(inp, T)
    dummy = [np.zeros((D, T), np.float32)] * 2
    mB = inputs_B(inp, T, dummy, [np.zeros((T, 512), np.float32)] * 8)
    wmod0 = np.ascontiguousarray(wpieces(inp["w_mod"][0], 512))
    wmod1 = np.ascontiguousarray(wpieces(inp["w_mod"][1], 512))
    wout1 = kcp(inp["rwkv_w_out"][0])
    maps = []
    for core in range(8):
        a, b = mA[core], mB[core]
        m = {"x": a["x"], "ca": a["ca"], "ident": a["ident"], "bones": a["bones"], "wmod0": wmod0, "wmod1": wmod1,
             "bmod0": a["bmod"], "bmod1": b["bmod1"], "wout1": wout1}
        for k in ("normw", "wqkz", "wv", "wf", "bf", "qkw", "masks"):
            m["A_" + k] = a[k]
        for k in ("normw", "wout", "mu", "w4", "l1", "w2", "a2", "v2", "pvec", "lnx", "sel"):
            m["B_" + k] = b[k]
        maps.append(m)
    return maps


def run_fused(inp, T):
    inp = {k: np.asarray(v, dtype=np.float32) for k, v in inp.items()}
    res = run_bass_kernel_spmd(build_F(T), inputs_F(inp, T), core_ids=list(range(8))).results
    return np.stack([res[0]["out"], res[4]["out"]], 0)


def kernel(**inputs):
    return run_fused(inputs, 16384)
```
